# Optimizing a Trainium2 kernel written in Bass

```python
import math
import jax, jax.numpy as jnp
from jax import lax
import numpy as np

D_MODEL = 1024
BATCH = 4
SEQ = 8192
DEPTH = 1

HEAD_DIM = 64
MOBA_HEADS = 6
DSA_HEADS = 6
MEM_HEADS = 4
MOBA_W = MOBA_HEADS * HEAD_DIM
DSA_W = DSA_HEADS * HEAD_DIM
MEM_W = MEM_HEADS * HEAD_DIM
IDX_HEADS = 8
IDX_DIM = 64
MOBA_BLOCK = 256
MOBA_TOPK = 3
DSA_TOPK = 256
N_MEM = 256
REL_BUCKETS = 32
REL_MAX_DIST = 128
Q_CHUNK = 128
EPS = 1e-6
IN_SIZES = (MOBA_W,) * 4 + (DSA_W,) * 4 + (IDX_HEADS * IDX_DIM, IDX_DIM, IDX_HEADS) + (MEM_W, MEM_W) + (D_MODEL,) * 3
IN_WIDTH = sum(IN_SIZES)
BRANCH_WIDTH = MOBA_W + DSA_W + MEM_W

kernel_name = 'hybrid_moba_dsa_memory_gated_block'


def rmsnorm(x, g):
    xf = x.astype(jnp.float32)
    y = xf * lax.rsqrt(jnp.mean(xf * xf, axis=-1, keepdims=True) + EPS)
    return (y * g.astype(jnp.float32)).astype(x.dtype)


def rel_bucket(dist):
    n = jnp.maximum(dist, 0)
    exact = REL_BUCKETS // 2
    nf = jnp.maximum(n, 1).astype(jnp.float32)
    large = exact + (jnp.log(nf / exact) / math.log(REL_MAX_DIST / exact) * (REL_BUCKETS - exact)).astype(jnp.int32)
    return jnp.where(n < exact, n, jnp.minimum(large, REL_BUCKETS - 1))


def split_cols(t, sizes):
    outs, off = [], 0
    for s in sizes:
        outs.append(t[..., off:off + s])
        off += s
    return outs


def masked_softmax(logits, mask):
    return jax.nn.softmax(jnp.where(mask, logits, -jnp.inf), axis=-1)


def moba_attention(q, k, v, bias_hb):
    b, t_len, h, dh = q.shape
    nb = -(-t_len // MOBA_BLOCK)
    padw = ((0, 0), (0, nb * MOBA_BLOCK - t_len), (0, 0), (0, 0))
    k_bh = jnp.pad(k, padw).reshape(b, nb, MOBA_BLOCK, h, dh).transpose(0, 3, 1, 2, 4)
    v_bh = jnp.pad(v, padw).reshape(b, nb, MOBA_BLOCK, h, dh).transpose(0, 3, 1, 2, 4)
    k_mean = jnp.mean(k_bh.astype(jnp.float32), axis=3)
    n_sel = min(MOBA_TOPK, nb)
    n_past = n_sel * MOBA_BLOCK
    scale = dh ** -0.5
    n_chunks = t_len // Q_CHUNK
    q_c = q.reshape(b, n_chunks, Q_CHUNK, h, dh).transpose(1, 0, 3, 2, 4)
    blk = jnp.arange(nb)
    offs = jnp.arange(MOBA_BLOCK)
    h_idx = jnp.arange(h)[None, :, None, None, None]
    gather = jax.vmap(jax.vmap(lambda kb, ib: kb[ib]))

    def one_chunk(args):
        ci, qc = args
        t = ci * Q_CHUNK + jnp.arange(Q_CHUNK)
        own = (ci * Q_CHUNK) // MOBA_BLOCK
        gate = jnp.einsum('bhqd,bhnd->bhqn', qc.astype(jnp.float32), k_mean)
        gate = jnp.where(blk < own, gate, -jnp.inf)
        _, sel = lax.top_k(gate, n_sel)
        k_sel = gather(k_bh, sel)
        v_sel = gather(v_bh, sel)
        lp = jnp.einsum('bhqd,bhqnsd->bhqns', qc, k_sel, preferred_element_type=jnp.float32) * scale
        pos = sel[..., None] * MOBA_BLOCK + offs
        lp = lp + bias_hb[h_idx, rel_bucket(t[:, None, None] - pos)]
        mp = jnp.broadcast_to((sel < own)[..., None], lp.shape)
        k_own = lax.dynamic_index_in_dim(k_bh, own, axis=2, keepdims=False)
        v_own = lax.dynamic_index_in_dim(v_bh, own, axis=2, keepdims=False)
        lo = jnp.einsum('bhqd,bhsd->bhqs', qc, k_own, preferred_element_type=jnp.float32) * scale
        dist = t[:, None] - (own * MOBA_BLOCK + offs)[None, :]
        lo = lo + bias_hb[:, rel_bucket(dist)][None]
        mo = jnp.broadcast_to(dist >= 0, lo.shape)
        logits = jnp.concatenate([lp.reshape(b, h, Q_CHUNK, n_past), lo], axis=-1)
        mask = jnp.concatenate([mp.reshape(b, h, Q_CHUNK, n_past), mo], axis=-1)
        p = masked_softmax(logits, mask).astype(v.dtype)
        p_past = p[..., :n_past].reshape(b, h, Q_CHUNK, n_sel, MOBA_BLOCK)
        return (jnp.einsum('bhqns,bhqnsd->bqhd', p_past, v_sel)
                + jnp.einsum('bhqs,bhsd->bqhd', p[..., n_past:], v_own))

    out = lax.map(one_chunk, (jnp.arange(n_chunks), q_c))
    return out.transpose(1, 0, 2, 3, 4).reshape(b, t_len, h * dh)


def dsa_attention(q, k, v, iq, ik, iw, bias_hb):
    b, t_len, h, dh = q.shape
    n_top = min(DSA_TOPK, t_len // 4)
    scale = dh ** -0.5
    n_chunks = t_len // Q_CHUNK
    q_c = q.reshape(b, n_chunks, Q_CHUNK, h, dh).transpose(1, 0, 2, 3, 4)
    iq_c = iq.reshape(b, n_chunks, Q_CHUNK, IDX_HEADS, IDX_DIM).transpose(1, 0, 2, 3, 4)
    iw_c = iw.reshape(b, n_chunks, Q_CHUNK, IDX_HEADS).transpose(1, 0, 2, 3)
    keys = jnp.arange(t_len)
    gather = jax.vmap(lambda a, i: a[i])

    def one_chunk(args):
        ci, qc, iqc, iwc = args
        t = ci * Q_CHUNK + jnp.arange(Q_CHUNK)
        rel = jax.nn.relu(jnp.einsum('bqhd,bsd->bqhs', iqc, ik, preferred_element_type=jnp.float32) * IDX_DIM ** -0.5)
        score = jnp.einsum('bqh,bqhs->bqs', iwc.astype(jnp.float32) * IDX_HEADS ** -0.5, rel)
        score = jnp.where(keys[None, None, :] <= t[None, :, None], score, -jnp.inf)
        _, idx = lax.top_k(score, n_top)
        k_sel = gather(k, idx)
        v_sel = gather(v, idx)
        logits = jnp.einsum('bqhd,bqkhd->bhqk', qc, k_sel, preferred_element_type=jnp.float32) * scale
        dist = t[None, :, None] - idx
        logits = logits + jnp.moveaxis(bias_hb[:, rel_bucket(dist)], 0, 1)
        mask = jnp.broadcast_to((dist >= 0)[:, None], logits.shape)
        p = masked_softmax(logits, mask).astype(v.dtype)
        return jnp.einsum('bhqk,bqkhd->bqhd', p, v_sel)

    out = lax.map(one_chunk, (jnp.arange(n_chunks), q_c, iq_c, iw_c))
    return out.transpose(1, 0, 2, 3, 4).reshape(b, t_len, h * dh)


def memory_attention(q, mem_k, mem_v):
    logits = jnp.einsum('bthd,bmhd->bhtm', q, mem_k, preferred_element_type=jnp.float32) * HEAD_DIM ** -0.5
    p = jax.nn.softmax(logits, axis=-1).astype(mem_v.dtype)
    return jnp.einsum('bhtm,bmhd->bthd', p, mem_v)


def hybrid_layer(x, mem, norm_gain, w_in, rel_bias, mem_norm_gain, w_mem_kv, w_branch, w_out):
    b, t_len, _ = x.shape
    n_mem = mem.shape[1]
    heads = lambda a: a.reshape(a.shape[0], a.shape[1], -1, HEAD_DIM)
    h = rmsnorm(x, norm_gain)
    proj = jnp.einsum('btd,dp->btp', h, w_in)
    (qa, ka, va, za, qb, kb, vb, zb, iq, ik, iw, qm, zm, ga, gb, gm) = split_cols(proj, IN_SIZES)
    bias_a = rel_bias[:, :MOBA_HEADS].T
    bias_b = rel_bias[:, MOBA_HEADS:].T
    ya = moba_attention(heads(qa), heads(ka), heads(va), bias_a) * jax.nn.silu(za)
    yb = dsa_attention(heads(qb), heads(kb), heads(vb),
                       iq.reshape(b, t_len, IDX_HEADS, IDX_DIM), ik, iw, bias_b) * jax.nn.silu(zb)
    mem_n = rmsnorm(mem, mem_norm_gain)
    mk, mv = split_cols(jnp.einsum('bmd,dp->bmp', mem_n, w_mem_kv), (MEM_W, MEM_W))
    ym = memory_attention(heads(qm), heads(mk), heads(mv)).reshape(b, t_len, MEM_W) * jax.nn.silu(zm)
    pa = w_branch[:MOBA_W]
    pb = w_branch[MOBA_W:MOBA_W + DSA_W]
    pm = w_branch[MOBA_W + DSA_W:]
    merged = (jax.nn.sigmoid(ga) * jnp.einsum('btc,cd->btd', ya, pa)
              + jax.nn.sigmoid(gb) * jnp.einsum('btc,cd->btd', yb, pb)
              + jax.nn.sigmoid(gm) * jnp.einsum('btc,cd->btd', ym, pm))
    return x + jnp.einsum('btd,de->bte', merged, w_out)


def setup_inputs(seed: int = 0) -> dict:
    key = jax.random.key(seed)
    ks = jax.random.split(key, 10)
    f = jnp.float32
    nrm = jax.random.normal
    return {
        'x': nrm(ks[0], (BATCH, SEQ, D_MODEL), f),
        'mem': nrm(ks[1], (BATCH, N_MEM, D_MODEL), f),
        'norm_gain': 1.0 + 0.02 * nrm(ks[2], (DEPTH, D_MODEL), f),
        'w_in': nrm(ks[3], (DEPTH, D_MODEL, IN_WIDTH), f) * D_MODEL ** -0.5,
        'rel_bias': 0.5 * nrm(ks[4], (REL_BUCKETS, MOBA_HEADS + DSA_HEADS), f),
        'mem_norm_gain': 1.0 + 0.02 * nrm(ks[5], (DEPTH, D_MODEL), f),
        'w_mem_kv': nrm(ks[6], (DEPTH, D_MODEL, 2 * MEM_W), f) * D_MODEL ** -0.5,
        'w_branch': nrm(ks[7], (DEPTH, BRANCH_WIDTH, D_MODEL), f) * MOBA_W ** -0.5,
        'w_out': nrm(ks[8], (DEPTH, D_MODEL, D_MODEL), f) * D_MODEL ** -0.5,
        'final_norm_gain': 1.0 + 0.02 * nrm(ks[9], (D_MODEL,), f),
    }


def reference(x, mem, norm_gain, w_in, rel_bias, mem_norm_gain, w_mem_kv, w_branch, w_out, final_norm_gain):
    for layer in range(DEPTH):
        x = hybrid_layer(x, mem, norm_gain[layer], w_in[layer], rel_bias, mem_norm_gain[layer],
                         w_mem_kv[layer], w_branch[layer], w_out[layer])
    return rmsnorm(x, final_norm_gain)
```

```python
import math
from contextlib import ExitStack

import numpy as np
import ml_dtypes

import concourse.bass as bass
import concourse.mybir as mybir
from concourse.bass_utils import run_bass_kernel_spmd

F32 = mybir.dt.float32
BF16 = mybir.dt.bfloat16
AF = mybir.ActivationFunctionType
ALU = mybir.AluOpType
AX = mybir.AxisListType

T = 8192
D = 1024
NOWN = 4096
NCH = 16
NEG = -30000.0
EPS = 1e-6
NBIS = 14
ARENA = 53200

OFF = {}
_o = 0
for _n, _s in [("qa", 384), ("ka", 384), ("va", 384), ("za", 384), ("qb", 384), ("kb", 384), ("vb", 384),
               ("zb", 384), ("iq", 512), ("ik", 64), ("iw", 8), ("qm", 256), ("zm", 256), ("ga", 1024),
               ("gb", 1024), ("gm", 1024)]:
    OFF[_n] = (_o, _o + _s)
    _o += _s
assert _o == 7240


class Sched:
    ENG = ("pe", "act", "dve", "pool", "sp")
    HND = {"pe": "tensor", "act": "scalar", "dve": "vector", "pool": "gpsimd", "sp": "sync"}

    def __init__(self, nc, stack):
        self.nc = nc
        self.stack = stack
        self.q = {e: [] for e in self.ENG}
        self.cnt = {e: 0 for e in self.ENG}
        self.semobj = {}
        for e in self.ENG:
            self.semobj["prog_" + e] = stack.enter_context(nc.semaphore("prog_" + e))
        self.dcount = {}
        self.lastw = {}
        self.readers = {}
        self.waited = {e: {} for e in self.ENG}

    def _deps(self, reads, writes):
        deps = []
        for k in reads:
            d = self.lastw.get(k)
            if d is not None:
                deps.append(d)
        for k in writes:
            d = self.lastw.get(k)
            if d is not None:
                deps.append(d)
            deps.extend(self.readers.get(k, ()))
        return deps

    def _commit(self, dep, reads, writes):
        for k in writes:
            self.lastw[k] = dep
            self.readers[k] = []
        for k in reads:
            self.readers.setdefault(k, []).append(dep)

    def _waits(self, eng, deps):
        need = {}
        for (sk, val, deng) in deps:
            if deng == eng and eng == "pe":
                continue
            if self.waited[eng].get(sk, 0) >= val:
                continue
            if need.get(sk, 0) < val:
                need[sk] = val
        for sk, val in need.items():
            self.waited[eng][sk] = val
        return list(need.items())

    def op(self, eng, fn, reads=(), writes=()):
        writes = list(writes) + [k for k in reads if isinstance(k, str) and k.startswith("ps") and k not in writes]
        waits = self._waits(eng, self._deps(reads, writes))
        self.cnt[eng] += 1
        sk = "prog_" + eng
        self.q[eng].append((waits, fn, (sk, 1)))
        dep = (sk, self.cnt[eng], eng)
        self._commit(dep, reads, writes)
        return dep

    def dma(self, eng, slot, fn, reads=(), writes=()):
        sk = "d_" + slot
        if sk not in self.semobj:
            self.semobj[sk] = self.stack.enter_context(self.nc.semaphore(sk))
            self.dcount[sk] = 0
        waits = self._waits(eng, self._deps(reads, writes))
        self.dcount[sk] += 16
        self.q[eng].append((waits, fn, (sk, 16)))
        dep = (sk, self.dcount[sk], "dma")
        self._commit(dep, reads, writes)
        return dep

    def barrier(self):
        deps = [("prog_" + e, self.cnt[e], e + "_b") for e in self.ENG if self.cnt[e] > 0]
        deps += [(sk, v, "dma") for sk, v in self.dcount.items() if v > 0]
        for e in self.ENG:
            waits = self._waits(e, [d for d in deps if d[0] != "prog_" + e])
            if waits:
                self.q[e].append((waits, None, None))

    def final_wait(self, eng, keys):
        deps = [self.lastw[k] for k in keys if k in self.lastw]
        waits = self._waits(eng, deps)
        self.q[eng].append((waits, None, None))

    def emit(self, block):
        def mk(ename):
            items = self.q[ename]

            def body(e):
                for waits, fn, inc in items:
                    for sk, val in waits:
                        e.wait_ge(self.semobj[sk], val)
                    if fn is not None:
                        fn(e).then_inc(self.semobj[inc[0]], inc[1])
            return body
        for ename in self.ENG:
            getattr(block, self.HND[ename])(mk(ename))


class Arena:
    def __init__(self, t, ncols):
        self.t = t
        self.n = ncols
        self.top = 0

    def f32(self, cols, parts=128):
        cols_al = (cols + 15) // 16 * 16
        off = self.top
        self.top += cols_al
        assert self.top <= self.n, ("arena overflow", self.top)
        return self.t[0:parts, off:off + cols]

    def bf16(self, cols, parts=128):
        c32 = ((cols + 1) // 2 + 15) // 16 * 16
        off = self.top
        self.top += c32
        assert self.top <= self.n, ("arena overflow", self.top)
        return self.t[0:parts, off:off + c32].bitcast(BF16)[:, 0:cols]


def build_program(dbg=False, nchunks=NCH, phases=("p1", "p2", "p3"), nt1a=16, nt1b=8):
    nc = bass.Bass("TRN2", target_bir_lowering=False)

    def din(name, shape, dt=F32):
        return nc.dram_tensor(name, list(shape), dt, kind="ExternalInput").ap()

    skind = "ExternalOutput" if dbg else "Internal"

    def dscr(name, shape, dt):
        return nc.dram_tensor(name, list(shape), dt, kind=skind).ap()

    xall = din("xall", [T, D])
    xown = din("xown", [NOWN, D])
    memx = din("memx", [256, D])
    gain = din("gain", [128, 8])
    mgain = din("mgain", [128, 8])
    fgain = din("fgain", [128, D])
    w_kvT = din("w_kvT", [D, 832])
    w_v = din("w_v", [D, 768])
    w_qT = din("w_qT", [D, 1536])
    w_iw = din("w_iw", [D, 8])
    w_zg = din("w_zg", [D, 4096])
    w_mkv = din("w_mkv", [D, 512])
    w_br = din("w_br", [D, D])
    w_o = din("w_o", [D, D])
    tabraw = din("tabraw", [12, 128, 5, 256])
    cmt = din("cmt", [128, 5, 256])
    b31 = din("b31", [128, 12])
    cmask = din("cmask", [128, 2, 512], BF16)
    pastneg = din("pastneg", [128, 16, 32], BF16)
    notown = din("notown", [128, 16, 32], BF16)
    identd = din("ident", [128, 128], BF16)
    Ed = din("E", [32, 32 * 128], BF16)

    KT = dscr("KT", [832, T], BF16)
    VS = dscr("VS", [12, 128, 64 * 65], BF16)
    QT = dscr("QT", [1536, NOWN], BF16)
    YS = dscr("YS", [NOWN, D], F32)
    out = nc.dram_tensor("out", [NOWN, D], F32, kind="ExternalOutput").ap()

    with ExitStack() as st:
        S = Sched(nc, st)
        arena_t = st.enter_context(nc.sbuf_tensor("arena", [128, ARENA], F32))
        ps = [st.enter_context(nc.psum_tensor("ps%d" % i, [128, 512], F32)) for i in range(8)]
        A = Arena(arena_t, ARENA)

        ident = A.bf16(128)
        gain_sb = A.f32(8)
        mgain_sb = A.f32(8)
        iw_all = A.f32(32 * 8)
        kms = A.f32(3 * 32)
        S.dma("sp", "c0", lambda e: e.dma_start(out=ident, in_=identd), writes=["ident"])
        S.dma("sp", "c1", lambda e: e.dma_start(out=gain_sb, in_=gain), writes=["gain"])
        S.dma("sp", "c2", lambda e: e.dma_start(out=mgain_sb, in_=mgain), writes=["mgain"])
        S.op("pool", lambda e: e.memset(kms, 0.0), [], ["kms0"])
        base_top = A.top

        evac_rr = [0]

        def evac(out_ap, in_ap, reads, writes, scale=None, eng=None):
            if eng is None:
                eng = ("act", "dve")[evac_rr[0] % 2]
                evac_rr[0] += 1
            if eng == "act":
                if scale is None:
                    S.op("act", lambda e: e.activation(out=out_ap, in_=in_ap, func=AF.Copy), reads, writes)
                else:
                    S.op("act", lambda e: e.activation(out=out_ap, in_=in_ap, func=AF.Copy, scale=float(scale)),
                         reads, writes)
            else:
                if scale is None:
                    S.op("dve", lambda e: e.tensor_copy(out=out_ap, in_=in_ap), reads, writes)
                else:
                    S.op("dve", lambda e: e.tensor_scalar(out=out_ap, in0=in_ap, scalar1=float(scale), scalar2=None,
                                                          op0=ALU.mult), reads, writes)

        def load_weight(wd, ncols, dst, gain_ap, tag, stg):
            for kc in range(8):
                sb = stg[kc % 2]
                sk = "wstg%d" % (kc % 2)
                S.dma("sp", sk, lambda e, kc=kc, sb=sb: e.dma_start(out=sb[:, 0:ncols], in_=wd[kc * 128:(kc + 1) * 128, :]),
                      writes=[sk])
                if gain_ap is None:
                    evac(dst[:, kc, :], sb[:, 0:ncols], [sk], [(tag, kc)])
                else:
                    g = gain_ap[:, kc:kc + 1]
                    if kc % 2 == 0:
                        S.op("act", lambda e, kc=kc, sb=sb, g=g: e.activation(out=dst[:, kc, :], in_=sb[:, 0:ncols],
                                                                           func=AF.Copy, scale=g),
                             [sk, "gain", "mgain"], [(tag, kc)])
                    else:
                        S.op("dve", lambda e, kc=kc, sb=sb, g=g: e.tensor_scalar(out=dst[:, kc, :], in0=sb[:, 0:ncols],
                                                                              scalar1=g, scalar2=None, op0=ALU.mult),
                             [sk, "gain", "mgain"], [(tag, kc)])

        def rms_tile(x_ap, nsub, xs_out, junk, ss, ms, rstd, rkeys, tag):
            for a in range(nsub):
                S.op("act", lambda e, a=a: e.activation(out=junk, in_=x_ap[:, a, :], func=AF.Square,
                                                       accum_out=ss[:, a:a + 1]),
                     rkeys, [tag + "junk", (tag + "ss", a)])
            S.op("dve", lambda e: e.tensor_scalar(out=ms[:, 0:nsub], in0=ss[:, 0:nsub], scalar1=1.0 / D, scalar2=EPS,
                                                  op0=ALU.mult, op1=ALU.add),
                 [(tag + "ss", a) for a in range(nsub)], [tag + "ms"])
            S.op("act", lambda e: e.activation(out=ms[:, 0:nsub], in_=ms[:, 0:nsub], func=AF.Sqrt),
                 [tag + "ms"], [tag + "ms"])
            S.op("dve", lambda e: e.reciprocal(out=rstd[:, 0:nsub], in_=ms[:, 0:nsub]), [tag + "ms"], [tag + "rstd"])
            if xs_out is not None:
                for a in range(nsub):
                    S.op("dve", lambda e, a=a: e.tensor_scalar(out=xs_out[:, a, :], in0=x_ap[:, a, :],
                                                              scalar1=rstd[:, a:a + 1], scalar2=None, op0=ALU.mult),
                         rkeys + [tag + "rstd"], [(tag + "xs", a)])

        def proj_phase(xsrc, ntiles, wT, featT, wtok, tok_fn, tag):
            m0 = A.top
            xt = [A.f32(4 * D).rearrange("p (a d) -> p a d", a=4) for _ in range(2)]
            xs = A.bf16(4 * D).rearrange("p (a d) -> p a d", a=4)
            junk = A.bf16(D)
            hT = [A.bf16(8 * 512).rearrange("p (k t) -> p k t", k=8) for _ in range(2)]
            stg = [A.bf16(512) for _ in range(3)]
            ss = A.f32(4)
            ms = A.f32(4)
            rstd = A.f32(4)
            fb = 0
            for i in range(ntiles):
                b = i % 2
                xk = tag + "xt%d" % b
                S.dma("sp", xk, lambda e, i=i, b=b: e.dma_start(
                    out=xt[b], in_=xsrc[i * 512:(i + 1) * 512, :].rearrange("(a p) d -> p a d", p=128)), writes=[xk])
                rms_tile(xt[b], 4, xs, junk, ss, ms, rstd, [xk], tag)
                hk = tag + "hT%d" % b
                for kc in range(8):
                    pb = 6 + (kc % 2)
                    pT = ps[pb][:].bitcast(BF16)[:, 0:512].rearrange("p (a t) -> p a t", a=4)
                    for a in range(4):
                        S.op("pe", lambda e, a=a, kc=kc, pT=pT: e.transpose(out=pT[:, a, :],
                                                                          in_=xs[:, a, kc * 128:(kc + 1) * 128],
                                                                          identity=ident),
                             [(tag + "xs", a), "ident"], ["ps%d" % pb])
                    evac(hT[b][:, kc, :], ps[pb][:].bitcast(BF16)[:, 0:512], ["ps%d" % pb], [(hk, kc)])
                hkeys = [(hk, kc) for kc in range(8)]
                for (c0, ncol, r0, scale, kblk, dst) in featT:
                    pb = fb % 3
                    fb += 1
                    pk = "ps%d" % pb
                    for kc in range(8):
                        S.op("pe", lambda e, kc=kc, pb=pb, c0=c0, ncol=ncol, b=b: e.matmul(
                            ps[pb][0:ncol, :], lhsT=wT[:, kc, c0:c0 + ncol], rhs=hT[b][:, kc, :],
                            start=(kc == 0), stop=(kc == 7)),
                             [(tag + "wT", kc), (hk, kc)], [pk])
                    sb = stg[fb % 3]
                    sk = tag + "stg%d" % (fb % 3)
                    evac(sb[0:ncol, :], ps[pb][0:ncol, :], [pk], [sk], scale=scale)
                    if kblk is not None:
                        S.op("dve", lambda e, pb=pb, kblk=kblk, i=i: e.tensor_reduce(
                            out=kms[:, kblk * 32 + 2 * i: kblk * 32 + 2 * i + 2],
                            in_=ps[pb][:, :].rearrange("p (n s) -> p n s", s=256), axis=AX.X, op=ALU.add),
                             [pk, "kms0"], [("kms", kblk, i), pk])
                    S.dma("pool", sk, lambda e, sb=sb, r0=r0, ncol=ncol, i=i, dst=dst: e.dma_start(
                        out=dst[r0:r0 + ncol, i * 512:(i + 1) * 512], in_=sb[0:ncol, :]), reads=[sk], writes=[(tag + "featdst", r0, i)])
                for a in range(4):
                    tok_fn(i, a, b, hk, hT[b])
            A.top = m0

        if "p1" in phases:
            m1 = A.top
            wkvT = A.bf16(8 * 832).rearrange("p (k n) -> p k n", k=8)
            wv = A.bf16(8 * 768).rearrange("p (k n) -> p k n", k=8)
            wstg = [A.f32(1536) for _ in range(2)]
            load_weight(w_kvT, 832, wkvT, gain_sb, "p1awT", wstg)
            load_weight(w_v, 768, wv, gain_sb, "p1awv", wstg)
            vstg = [A.bf16(4 * 12 * 65).rearrange("p (h a d) -> p h a d", a=4, h=12) for _ in range(2)]
            for vb_ in range(2):
                S.op("pool", lambda e, vb_=vb_: e.memset(vstg[vb_], 1.0), [], ["vstg%d" % vb_])
            featT = [(blk * 128, 128, blk * 128, None, (blk if blk < 3 else None), KT) for blk in range(6)]
            featT.append((768, 64, 768, None, None, KT))

            def tok_v(i, a, b, hk, hTb):
                vb_ = i % 2
                vk = "vstg%d" % vb_
                for g, (c0, ncol) in enumerate([(0, 512), (512, 256)]):
                    pb = 3 + (a * 2 + g) % 3
                    pk = "ps%d" % pb
                    for kc in range(8):
                        S.op("pe", lambda e, kc=kc, pb=pb, c0=c0, ncol=ncol, a=a: e.matmul(
                            ps[pb][:, 0:ncol], lhsT=hTb[:, kc, a * 128:(a + 1) * 128], rhs=wv[:, kc, c0:c0 + ncol],
                            start=(kc == 0), stop=(kc == 7)),
                             [("p1awv", kc), (hk, kc)], [pk])
                    h0 = c0 // 64
                    nh = ncol // 64
                    evac(vstg[vb_][:, h0:h0 + nh, a, 0:64], ps[pb][:, 0:ncol].rearrange("p (h d) -> p h d", d=64),
                         [pk, vk], [(vk, a, g)])
                if a == 3:
                    S.dma("pool", vk, lambda e, i=i, vb_=vb_: e.dma_start(
                        out=VS[:, :, i * 4 * 65:(i + 1) * 4 * 65].rearrange("h p x -> p h x"),
                        in_=vstg[vb_].rearrange("p h a d -> p h (a d)")),
                        reads=[(vk, aa, g) for aa in range(4) for g in range(2)], writes=[("VS", i)])

            proj_phase(xall, nt1a, wkvT, featT, wv, tok_v, "p1a")
            A.top = m1
            S.barrier()

            wqT = A.bf16(8 * 1536).rearrange("p (k n) -> p k n", k=8)
            wiw = A.bf16(8 * 8).rearrange("p (k n) -> p k n", k=8)
            wstg = [A.f32(1536) for _ in range(2)]
            load_weight(w_qT, 1536, wqT, gain_sb, "p1bwT", wstg)
            load_weight(w_iw, 8, wiw, gain_sb, "p1bwiw", wstg)
            featT = []
            for blk in range(12):
                scale = None if 6 <= blk < 10 else 0.125
                featT.append((blk * 128, 128, blk * 128, scale, None, QT))

            def tok_iw(i, a, b, hk, hTb):
                pb = 3 + a % 3
                pk = "ps%d" % pb
                for kc in range(8):
                    S.op("pe", lambda e, kc=kc, pb=pb, a=a: e.matmul(
                        ps[pb][:, 0:8], lhsT=hTb[:, kc, a * 128:(a + 1) * 128], rhs=wiw[:, kc, :],
                        start=(kc == 0), stop=(kc == 7)),
                         [("p1bwiw", kc), (hk, kc)], [pk])
                sl = 4 * i + a
                evac(iw_all[:, sl * 8:(sl + 1) * 8], ps[pb][:, 0:8], [pk], [("iw", sl)])

            proj_phase(xown, nt1b, wqT, featT, wiw, tok_iw, "p1b")
            A.top = m1
            S.barrier()


        if "p2" in phases:
            m2 = A.top
            ikT = A.bf16(T)
            tabs = A.bf16(12 * 5 * 256).rearrange("p (h i t) -> p h i t", h=12, i=5)
            Esb = A.bf16(32 * 128).rearrange("p (n s) -> p n s", n=32)
            b31_sb = A.f32(12)
            cmask_sb = A.bf16(2 * 512).rearrange("p (s k) -> p s k", s=2)
            pastneg_sb = A.bf16(16 * 32).rearrange("p (m n) -> p m n", m=16)
            notown_sb = A.bf16(16 * 32).rearrange("p (m n) -> p m n", m=16)
            kmbf = A.bf16(3 * 32).rearrange("p (b n) -> p b n", b=3)
            mbT = A.bf16(64 * 256).rearrange("p (j t) -> p j t", j=64)
            mbmT = A.bf16(6 * 256).rearrange("p (h t) -> p h t", h=6)
            iq_sb = A.bf16(8 * 256).rearrange("p (h t) -> p h t", h=8)
            qz = [A.bf16(nb_ * 512).rearrange("p (b t) -> p b t", b=nb_) for nb_ in (3, 3, 2)]
            gsb = A.f32(2 * 192).rearrange("p (a n) -> p a n", a=2)
            praw = [gsb.rearrange("p a n -> p (a n)")[:, 0:260]]
            ystg = [A.f32(2 * 384).rearrange("p (s d) -> p s d", s=2) for _ in range(2)]
            mkT = A.bf16(2 * 256).rearrange("p (b t) -> p b t", b=2)
            memV = A.bf16(2 * 4 * 65).rearrange("p (j h d) -> p j h d", j=2, h=4)
            smalls = A.f32(48)
            rmax = smalls[:, 0:1]
            lo = smalls[:, 1:2]
            w0 = smalls[:, 2:3]
            mid = smalls[:, 3:4]
            cnt = smalls[:, 4:5]
            step = smalls[:, 5:6]
            gm = A.f32(6 * 32).rearrange("p (h n) -> p h n", h=6)
            mx8 = A.f32(6 * 8).rearrange("p (h n) -> p h n", h=6)
            mbm = A.bf16(2 * 6 * 32).rearrange("p (s h n) -> p s h n", s=2, h=6)
            rden = smalls[:, 8:12]
            bmax = smalls[:, 16:32]
            bmin = smalls[:, 32:48]
            p16 = [A.bf16(512) for _ in range(7)]
            m_sc = A.top
            sc = A.f32(T)
            mb = A.bf16(T)
            dgw = A.bf16(8 * 128).rearrange("p (h t) -> p h t", h=8)
            khT = A.bf16(T)
            vh = [A.bf16(64 * 65) for _ in range(2)]
            mB = A.top
            A.top = m_sc

            S.op("pool", lambda e: e.memset(ikT, 0.0), [], ["ikT"])
            S.op("pool", lambda e: e.memset(Esb, 0.0), [], ["E"])
            S.op("pool", lambda e: e.memset(mbmT, 0.0), [], [("mbmT", 0), ("mbmT", 1)])
            S.op("pool", lambda e: e.memset(iq_sb, 0.0), [], ["iq"])
            S.dma("sp", "s_ik", lambda e: e.dma_start(out=ikT[0:64, 0:512 * nt1a], in_=KT[768:832, 0:512 * nt1a]), writes=["ikT"])
            S.dma("sp", "s_E", lambda e: e.dma_start(out=Esb[0:32], in_=Ed.rearrange("p (n s) -> p n s", n=32)), writes=["E"])
            S.dma("sp", "s_b31", lambda e: e.dma_start(out=b31_sb, in_=b31), writes=["b31"])
            S.dma("sp", "s_cm", lambda e: e.dma_start(out=cmask_sb, in_=cmask), writes=["cmask"])
            S.dma("sp", "s_pn", lambda e: e.dma_start(out=pastneg_sb, in_=pastneg), writes=["pastneg"])
            S.dma("sp", "s_no", lambda e: e.dma_start(out=notown_sb, in_=notown), writes=["notown"])
            cmt_sb = A.f32(5 * 256).rearrange("p (i t) -> p i t", i=5)
            tstg = [A.f32(5 * 256).rearrange("p (i t) -> p i t", i=5) for _ in range(2)]
            S.dma("sp", "s_cmt", lambda e: e.dma_start(out=cmt_sb, in_=cmt), writes=["cmt"])
            for hd in range(12):
                tk = "tstg%d" % (hd % 2)
                S.dma("sp", tk, lambda e, hd=hd: e.dma_start(out=tstg[hd % 2], in_=tabraw[hd]), writes=[tk])
                S.op("dve", lambda e, hd=hd: e.scalar_tensor_tensor(out=tabs[:, hd], in0=tstg[hd % 2],
                                                                   scalar=b31_sb[:, hd:hd + 1], in1=cmt_sb,
                                                                   op0=ALU.subtract, op1=ALU.add),
                     [tk, "b31", "cmt"], [("tabs", hd)])
            S.op("dve", lambda e: e.tensor_copy(out=kmbf, in_=kms.rearrange("p (b n) -> p b n", b=3)), [], ["kmbf"])
            for qi in range(3):
                S.op("pool", lambda e, qi=qi: e.memset(qz[qi], 0.0), [], ["qzero%d" % qi])
            xm = A.f32(2 * D).rearrange("p (a d) -> p a d", a=2)
            xsm = A.bf16(2 * D).rearrange("p (a d) -> p a d", a=2)
            junkm = A.bf16(D)
            hTm = A.bf16(8 * 256).rearrange("p (k t) -> p k t", k=8)
            wmk = A.bf16(8 * 512).rearrange("p (k n) -> p k n", k=8)
            ssm = A.f32(2)
            msm = A.f32(2)
            rsm = A.f32(2)
            wstg = [A.f32(512) for _ in range(2)]
            load_weight(w_mkv, 512, wmk, mgain_sb, "wmk", wstg)
            S.dma("sp", "s_xm", lambda e: e.dma_start(out=xm, in_=memx.rearrange("(a p) d -> p a d", p=128)), writes=["xm"])
            rms_tile(xm, 2, xsm, junkm, ssm, msm, rsm, ["xm"], "mem")
            for kc in range(8):
                pb = 3 + kc % 2
                pT = ps[pb][:].bitcast(BF16)[:, 0:256].rearrange("p (a t) -> p a t", a=2)
                for a in range(2):
                    S.op("pe", lambda e, a=a, kc=kc, pT=pT: e.transpose(out=pT[:, a, :], in_=xsm[:, a, kc * 128:(kc + 1) * 128],
                                                                      identity=ident),
                         [("memxs", a), "ident"], ["ps%d" % pb])
                evac(hTm[:, kc, :], ps[pb][:].bitcast(BF16)[:, 0:256], ["ps%d" % pb], [("hTm", kc)])
            for blk in range(2):
                pb = blk
                for kc in range(8):
                    S.op("pe", lambda e, kc=kc, pb=pb, blk=blk: e.matmul(ps[pb][:, 0:256], lhsT=wmk[:, kc, blk * 128:(blk + 1) * 128],
                                                                       rhs=hTm[:, kc, :], start=(kc == 0), stop=(kc == 7)),
                         [("wmk", kc), ("hTm", kc)], ["ps%d" % pb])
                evac(mkT[:, blk, :], ps[pb][:, 0:256], ["ps%d" % pb], [("mkT", blk)])
            S.op("pool", lambda e: e.memset(memV, 1.0), [], ["memV"])
            for j in range(2):
                pb = 5 + j
                for kc in range(8):
                    S.op("pe", lambda e, kc=kc, pb=pb, j=j: e.matmul(ps[pb][:, 0:256], lhsT=hTm[:, kc, j * 128:(j + 1) * 128],
                                                                   rhs=wmk[:, kc, 256:512], start=(kc == 0), stop=(kc == 7)),
                         [("wmk", kc), ("hTm", kc)], ["ps%d" % pb])
                evac(memV[:, j, :, 0:64], ps[pb][:, 0:256].rearrange("p (h d) -> p h d", d=64), ["ps%d" % pb, "memV"],
                     [("memV", j)])
            S.barrier()
            A.top = mB

            cnts = {"ps": 0, "ips": 0, "p16": 0, "r16": 0, "acc": 0, "y": 0}
            deferred = []

            def flush_deferred(keep=0):
                while len(deferred) > keep:
                    for f in deferred.pop(0):
                        f()

            ALLBR = (("dsa", 3, 384, 384, 6, 6, 384, 0), ("moba", 3, 0, 0, 0, 0, 0, 1), ("mem", 2, 1280, 0, 0, 0, 768, 2))

            def load_q(mc, br):
                (mode, npair, qrow0, krow0, vidx0, tab0, ycol0, qi) = br
                qk = "qz%d" % qi
                qsrc = QT[qrow0:qrow0 + 128 * npair, 256 * mc:256 * mc + 256].rearrange("(b two p) t -> two p b t", two=2, p=64)
                S.dma("sp", qk + "a", lambda e: e.dma_start(out=qz[qi][0:64, 0:npair, 0:256], in_=qsrc[0]),
                      reads=["qzero%d" % qi], writes=[(qk, 0)])
                S.dma("sp", qk + "b", lambda e: e.dma_start(out=qz[qi][64:128, 0:npair, 256:512], in_=qsrc[1]),
                      reads=["qzero%d" % qi], writes=[(qk, 1)])

            def gate_part1(m):
                load_q(m, ALLBR[1])
                mqk = [("qz1", 0), ("qz1", 1)]
                for slot in range(2):
                    for hd in range(6):
                        p0 = (hd % 2) * 64
                        c0 = (hd % 2) * 256 + slot * 128
                        gb_ = 7 if hd % 2 == 0 else 4
                        gc = (slot * 3 + hd // 2) * 32
                        S.op("pe", lambda e, hd=hd, p0=p0, c0=c0, gb_=gb_, gc=gc: e.matmul(
                            ps[gb_][:, gc:gc + 32], lhsT=qz[1][p0:p0 + 64, hd // 2, c0:c0 + 128],
                            rhs=kmbf[p0:p0 + 64, hd // 2, :], start=True, stop=True), mqk + ["kmbf"], ["ps%d" % gb_])
                for par, gb_ in ((0, 7), (1, 4)):
                    S.op("act", lambda e, par=par, gb_=gb_: e.activation(out=gsb[:, par, :], in_=ps[gb_][:, 0:192], func=AF.Copy),
                         ["ps%d" % gb_], ["gp"])
                for slot in range(2):
                    for hd in range(6):
                        gc = (slot * 3 + hd // 2) * 32
                        S.op("dve", lambda e, hd=hd, gc=gc: e.tensor_tensor(
                            out=gm[:, hd, :], in0=gsb[:, hd % 2, gc:gc + 32], in1=pastneg_sb[:, m, :], op=ALU.add),
                             ["gp", "pastneg"], [("gm", hd)])
                        S.op("dve", lambda e, hd=hd: e.max(out=mx8[:, hd, :], in_=gm[:, hd, :]), [("gm", hd)], [("mx8", hd)])
                        S.op("dve", lambda e, hd=hd, slot=slot: e.scalar_tensor_tensor(
                            out=mbm[:, slot, hd, :], in0=gm[:, hd, :], scalar=mx8[:, hd, 2:3], in1=notown_sb[:, m, :],
                            op0=ALU.is_lt, op1=ALU.mult), [("gm", hd), ("mx8", hd), "notown"], [("mbm", slot, hd)])

            def gate_part2():
                if True:
                    for slot in range(2):
                        pTm = ps[7][:].bitcast(BF16)[0:32, 0:6 * 128].rearrange("p (h t) -> p h t", h=6)
                        for hd in range(6):
                            S.op("pe", lambda e, hd=hd, slot=slot, pTm=pTm: e.transpose(out=pTm[:, hd, :], in_=mbm[:, slot, hd, :],
                                                                                     identity=ident),
                                 [("mbm", slot, hd), "ident"], ["ps7"])
                        evac(mbmT[0:32, :, slot * 128:(slot + 1) * 128], pTm, ["ps7"], [("mbmT", slot)], eng="act")

            def gen_attention(m, modes, gate_for=None):
                L = 512 * (m + 1)
                nj = 4 * (m + 1)
                t0 = 256 * m
                pob = 5
                pok = "ps5"
                branches = [br for br in ALLBR if br[0] in modes]
                for br in branches:
                    if br[0] != "moba":
                        load_q(m, br)
                if gate_for is not None:
                    gate_part1(gate_for)
                first_pair = True
                for (mode, npair, qrow0, krow0, vidx0, tab0, ycol0, qi) in branches:
                    nh = 2 * npair
                    qkeys = [("qz%d" % qi, 0), ("qz%d" % qi, 1)]
                    yb_ = cnts["y"] % 2
                    cnts["y"] += 1
                    njj = 2 if mode == "mem" else nj
                    nseg = 1 if njj < 4 else 4
                    jps = njj // nseg
                    for b in range(npair):
                        if gate_for is not None and b == 1 and first_pair:
                            gate_part2()
                            first_pair = False
                        if mode != "mem":
                            flush_deferred()
                            for sg_ in range(nseg):
                                c0, c1 = sg_ * jps * 128, (sg_ + 1) * jps * 128
                                S.dma("sp", "khTs%d" % sg_, lambda e, b=b, c0=c0, c1=c1, krow0=krow0: e.dma_start(
                                    out=khT[:, c0:c1], in_=KT[krow0 + 128 * b:krow0 + 128 * b + 128, c0:c1]),
                                    writes=[("khT", j_) for j_ in range(sg_ * jps, (sg_ + 1) * jps)])
                                for wh in range(2):
                                    v0, v1 = sg_ * jps * 65, (sg_ + 1) * jps * 65
                                    S.dma("sp", "vhs%d_%d" % (wh, sg_), lambda e, b=b, wh=wh, v0=v0, v1=v1, vidx0=vidx0: e.dma_start(
                                        out=vh[wh][:, v0:v1], in_=VS[vidx0 + 2 * b + wh, :, v0:v1]),
                                        writes=[("vh", wh, j_) for j_ in range(sg_ * jps, (sg_ + 1) * jps)])
                        q_ap = qz[qi][:, b, :]
                        for j in range(njj):
                            sb_ = cnts["ips"] % 4
                            cnts["ips"] += 1
                            sk = "ps%d" % sb_
                            pS = ps[sb_][:, :]
                            has_tab = (mode != "mem") and (j >= nj - 5)
                            if mode == "mem":
                                S.op("pe", lambda e, b=b, j=j, pS=pS, q_ap=q_ap: e.matmul(
                                    pS, lhsT=mkT[:, b, j * 128:(j + 1) * 128], rhs=q_ap, start=True, stop=True),
                                     [("mkT", b)] + qkeys, [sk])
                            else:
                                S.op("pe", lambda e, j=j, pS=pS, q_ap=q_ap: e.matmul(
                                    pS, lhsT=khT[:, j * 128:(j + 1) * 128], rhs=q_ap, start=True, stop=False),
                                     [("khT", j)] + qkeys, [sk])
                                if mode == "dsa":
                                    S.op("pe", lambda e, j=j, pS=pS, ht=has_tab: e.matmul(
                                        pS, lhsT=ident, rhs=mbT[:, j:j + 1, :].broadcast_to([128, 2, 256]), start=False, stop=(not ht)),
                                         ["ident", ("mbT", j // 4, 0), ("mbT", j // 4, 1)], [sk])
                                else:
                                    S.op("pe", lambda e, b=b, j=j, pS=pS, ht=has_tab: e.matmul(
                                        pS, lhsT=Esb[:, j // 2, :], rhs=mbmT[:, 2 * b:2 * b + 2, :], start=False, stop=(not ht)),
                                         ["E", ("mbmT", 0), ("mbmT", 1)], [sk])
                                if has_tab:
                                    S.op("pe", lambda e, b=b, j=j, pS=pS, tab0=tab0: e.matmul(
                                        pS, lhsT=ident, rhs=tabs[:, tab0 + 2 * b:tab0 + 2 * b + 2, j - (nj - 5), :], start=False, stop=True),
                                         ["ident", ("tabs", tab0 + 2 * b), ("tabs", tab0 + 2 * b + 1)], [sk])
                            pe_ = cnts["p16"] % 3
                            cnts["p16"] += 1
                            pk_ = "p16_%d" % pe_
                            S.op("act", lambda e, sb_=sb_, pe_=pe_: e.activation(out=p16[pe_], in_=ps[sb_][:, :], func=AF.Exp), [sk], [pk_])
                            flush_deferred(keep=1)

                            def pv(j=j, pe_=pe_, pk_=pk_, b=b, njj=njj, mode=mode):
                                for wh in range(2):
                                    for slot in range(2):
                                        if mode == "mem":
                                            rhs = memV[:, j, 2 * b + wh, :]
                                            rk = [("memV", 0), ("memV", 1)]
                                        else:
                                            rhs = vh[wh][:, j * 65:(j + 1) * 65]
                                            rk = [("vh", wh, j)]
                                        q4 = wh * 2 + slot
                                        S.op("pe", lambda e, q4=q4, rhs=rhs: e.matmul(
                                            ps[pob][:, q4 * 65:(q4 + 1) * 65], lhsT=p16[pe_][:, q4 * 128:(q4 + 1) * 128],
                                            rhs=rhs, start=(j == 0 and q4 == 0), stop=(j == njj - 1), skip_group_check=True),
                                             [pk_] + rk, [pok])
                            entry = [pv]
                            if j == njj - 1:
                                def norm(b=b, yb_=yb_, ycol0=ycol0, nh=nh, npair=npair, mode=mode):
                                    prb = 0
                                    prk = "gp"
                                    S.op("act", lambda e: e.activation(out=praw[prb], in_=ps[pob][:, 0:260], func=AF.Copy), [pok], [prk])
                                    S.op("dve", lambda e: e.reciprocal(
                                        out=rden, in_=praw[prb].rearrange("p (q d) -> p q d", q=4)[:, :, 64]), [prk], ["rden"])
                                    for wh in range(2):
                                        for slot in range(2):
                                            q4 = wh * 2 + slot
                                            hd = 2 * b + wh
                                            S.op("dve", lambda e, slot=slot, q4=q4, hd=hd: e.tensor_scalar(
                                                out=ystg[yb_][:, slot, 64 * hd:64 * hd + 64], in0=praw[prb][:, q4 * 65:q4 * 65 + 64],
                                                scalar1=rden[:, q4:q4 + 1], scalar2=None, op0=ALU.mult),
                                                 [prk, "rden"], [("ystg", yb_, slot, hd)])
                                    if b == npair - 1:
                                        ykeys = [("ystg", yb_, s_, h_) for s_ in range(2) for h_ in range(nh)]
                                        S.dma("pool", "ystore%d" % yb_, lambda e: e.dma_start(
                                            out=YS[t0:t0 + 256, ycol0:ycol0 + 64 * nh].rearrange("(s p) d -> p s d", p=128),
                                            in_=ystg[yb_][:, :, 0:64 * nh]), reads=ykeys, writes=ykeys + [("YS", m, mode)])
                                entry.append(norm)
                            deferred.append(entry)
                            yield 0.65
                    if mode == "dsa":
                        yield ("dsa_done",)

            def gen_indexer(m):
                L = 512 * (m + 1)
                nsg = m + 1
                t0 = 256 * m
                S.dma("sp", "iq", lambda e: e.dma_start(
                    out=iq_sb[0:64], in_=QT[768:1280, t0:t0 + 256].rearrange("(h d) t -> d h t", d=64)), writes=["iq"])
                sckeys = [("sc", sg) for sg in range(nsg)]
                for slot in range(2):
                    sl = 2 * m + slot
                    for hh in range(8):
                        S.op("dve", lambda e, hh=hh, sl=sl: e.tensor_scalar(
                            out=dgw[:, hh, :], in0=ident, scalar1=iw_all[:, sl * 8 + hh:sl * 8 + hh + 1], scalar2=None,
                            op0=ALU.mult), ["ident", ("iw", sl)], [("dg", hh)])
                    for sg in range(nsg):
                        blk = sc[:, sg * 512:(sg + 1) * 512]
                        ab = (4, 6)[cnts["acc"] % 2]
                        cnts["acc"] += 1
                        ak = "ps%d" % ab
                        pq = []
                        for hh in range(8):
                            pb = cnts["ips"] % 4
                            cnts["ips"] += 1
                            rb = 3 + cnts["r16"] % 4
                            cnts["r16"] += 1
                            rk_ = "p16_%d" % rb
                            S.op("pe", lambda e, hh=hh, sg=sg, pb=pb, slot=slot: e.matmul(
                                ps[pb][:, :], lhsT=iq_sb[:, hh, slot * 128:(slot + 1) * 128], rhs=ikT[:, sg * 512:(sg + 1) * 512],
                                start=True, stop=True), ["iq", "ikT"], ["ps%d" % pb])
                            if hh % 2 == 0:
                                S.op("act", lambda e, pb=pb, rb=rb: e.activation(out=p16[rb], in_=ps[pb][:, :], func=AF.Relu),
                                     ["ps%d" % pb], [rk_])
                            else:
                                S.op("dve", lambda e, pb=pb, rb=rb: e.tensor_scalar(out=p16[rb], in0=ps[pb][:, :], scalar1=0.0,
                                                                                   scalar2=None, op0=ALU.max),
                                     ["ps%d" % pb], [rk_])

                            def dmm(hh=hh, rb=rb, rk_=rk_, ab=ab, ak=ak):
                                S.op("pe", lambda e: e.matmul(ps[ab][:, :], lhsT=dgw[:, hh, :], rhs=p16[rb], start=(hh == 0), stop=(hh == 7)),
                                     [("dg", hh), rk_], [ak])
                            pq.append(dmm)
                            while len(pq) > 3:
                                pq.pop(0)()
                            yield ("s", 0.45)
                        while pq:
                            pq.pop(0)()
                        S.op("act", lambda e, blk=blk, ab=ab: e.activation(out=blk, in_=ps[ab][:, :], func=AF.Copy), [ak], [("sc", sg)])
                        S.op("dve", lambda e, blk=blk, sg=sg: e.tensor_reduce(out=bmax[:, sg:sg + 1], in_=blk, axis=AX.X, op=ALU.max),
                             [("sc", sg)], [("bmax", sg)])
                        S.op("dve", lambda e, blk=blk, sg=sg: e.tensor_reduce(out=bmin[:, sg:sg + 1], in_=blk, axis=AX.X, op=ALU.min),
                             [("sc", sg)], [("bmin", sg)])
                    S.op("dve", lambda e: e.tensor_reduce(out=rmax, in_=bmax[:, 0:nsg], axis=AX.X, op=ALU.max),
                         [("bmax", g_) for g_ in range(nsg)], ["rmax"])
                    S.op("dve", lambda e: e.tensor_reduce(out=lo, in_=bmin[:, 0:nsg], axis=AX.X, op=ALU.min),
                         [("bmin", g_) for g_ in range(nsg)], ["lo"])
                    S.op("dve", lambda e, slot=slot: e.tensor_tensor(out=sc[:, L - 512:L], in0=sc[:, L - 512:L],
                                                                   in1=cmask_sb[:, slot, :], op=ALU.add),
                         ["cmask"], [("sc", nsg - 1)])
                    S.op("dve", lambda e: e.tensor_tensor(out=w0, in0=rmax, in1=lo, op=ALU.subtract), ["rmax", "lo"], ["w0"])
                    yield ("b", 1.0)
                    for k in range(NBIS):
                        c = 2.0 ** -(k + 1)
                        S.op("dve", lambda e, c=c: e.scalar_tensor_tensor(out=mid, in0=w0, scalar=c, in1=lo, op0=ALU.mult,
                                                                        op1=ALU.add), ["w0", "lo"], ["mid"])
                        S.op("dve", lambda e: e.tensor_scalar(out=mb[:, 0:L], in0=sc[:, 0:L], scalar1=mid, scalar2=0.0,
                                                             op0=ALU.is_ge, op1=ALU.add, accum_out=cnt),
                             sckeys + ["mid"], ["mb", "cnt"])
                        S.op("dve", lambda e, c=c: e.tensor_scalar(out=step, in0=cnt, scalar1=255.5, scalar2=c, op0=ALU.is_ge,
                                                                  op1=ALU.mult), ["cnt"], ["step"])
                        S.op("dve", lambda e: e.scalar_tensor_tensor(out=lo, in0=step, scalar=w0, in1=lo, op0=ALU.mult,
                                                                    op1=ALU.add), ["step", "w0", "lo"], ["lo"])
                        yield ("b", L * 1.1e-3 + 0.6)
                    S.op("dve", lambda e: e.tensor_scalar(out=mb[:, 0:L], in0=sc[:, 0:L], scalar1=lo, scalar2=NEG,
                                                         op0=ALU.is_lt, op1=ALU.mult), sckeys + ["lo"], ["mb"])
                    yield ("b", L * 0.6e-3)
                    yield ("maskT", slot)

            def emit_maskT(m, slot):
                nj = 4 * (m + 1)
                for jg in range(nj // 4):
                    tb_ = (7, 6)[jg % 2]
                    tk_ = "ps%d" % tb_
                    pT = ps[tb_][:].bitcast(BF16)[:, 0:512].rearrange("p (a t) -> p a t", a=4)
                    for jj in range(4):
                        j = 4 * jg + jj
                        S.op("pe", lambda e, jj=jj, j=j, pT=pT: e.transpose(out=pT[:, jj, :], in_=mb[:, j * 128:(j + 1) * 128],
                                                                          identity=ident), ["mb", "ident"], [tk_])
                    evac(mbT[:, 4 * jg:4 * jg + 4, slot * 128:(slot + 1) * 128], pT, [tk_], [("mbT", jg, slot)])

            def drive(ga, gb, mb_chunk, a_total=0.0):
                Lb = 512 * (mb_chunk + 1)
                bis_total = 2 * (NBIS * (Lb * 1.1e-3 + 0.6) + 1.0 + Lb * 0.6e-3)
                score_total = 2 * (mb_chunk + 1) * 8 * 0.45
                ratio = max(0.0, (a_total - bis_total) / score_total) if gb is not None else 0.0
                ta = tb = 0.0
                a_done = ga is None
                dsa_done = ga is None
                b_done = gb is None
                b_wait = None
                while not (a_done and b_done):
                    if b_wait is not None and dsa_done:
                        emit_maskT(mb_chunk, b_wait)
                        b_wait = None
                        continue
                    pick_a = (not a_done) and (b_done or b_wait is not None or ta < tb)
                    if pick_a:
                        try:
                            c = next(ga)
                            if isinstance(c, tuple):
                                dsa_done = True
                            else:
                                ta += c
                        except StopIteration:
                            a_done = True
                            dsa_done = True
                            flush_deferred()
                    else:
                        try:
                            c = next(gb)
                            if c[0] == "maskT":
                                b_wait = c[1]
                            elif c[0] == "s":
                                tb += c[1] * ratio
                            else:
                                tb += c[1]
                        except StopIteration:
                            b_done = True

            def chain(*gens):
                for g in gens:
                    if g is not None:
                        yield from g

            def gate_now(m):
                gate_part1(m)
                gate_part2()
                yield 5.0

            drive(chain(gate_now(0), gen_attention(0, ("moba", "mem"))), gen_indexer(0), 0, a_total=5.0 + 16 * 0.65)
            for m in range(nchunks):
                nxt = m + 1 if m + 1 < nchunks else None
                ga = chain(gen_attention(m, ("dsa",), gate_for=nxt),
                           gen_attention(nxt, ("moba", "mem")) if nxt is not None else None)
                a_total = 0.65 * (3 * 4 * (m + 1) + (3 * 4 * (m + 2) + 4 if nxt is not None else 0))
                drive(ga, gen_indexer(nxt) if nxt is not None else None, m + 1, a_total=a_total)
            S.barrier()
            A.top = m2

        if "p3" in phases:
            m3 = A.top
            wzg = A.bf16(8 * 4096).rearrange("p (k n) -> p k n", k=8)
            wbr = A.bf16(8 * D).rearrange("p (k n) -> p k n", k=8)
            wo = A.bf16(8 * D).rearrange("p (k n) -> p k n", k=8)
            fg = A.f32(D)
            mw = A.top
            wstg = [A.f32(4096) for _ in range(2)]
            load_weight(w_zg, 4096, wzg, gain_sb, "wzg", wstg)
            load_weight(w_br, D, wbr, None, "wbr", wstg)
            load_weight(w_o, D, wo, None, "wo", wstg)
            S.dma("sp", "s_fg", lambda e: e.dma_start(out=fg, in_=fgain), writes=["fg"])
            S.barrier()
            A.top = mw
            xt3 = [A.f32(D).rearrange("p (a d) -> p a d", a=1) for _ in range(2)]
            yt3 = [A.f32(D) for _ in range(2)]
            xs3 = A.bf16(D).rearrange("p (a d) -> p a d", a=1)
            junk3 = A.bf16(D)
            junk4 = A.bf16(D)
            hT3 = A.bf16(8 * 128).rearrange("p (k t) -> p k t", k=8)
            zg2 = [A.f32(4096) for _ in range(2)]
            u3 = A.bf16(D)
            uT = A.bf16(8 * 128).rearrange("p (k t) -> p k t", k=8)
            merged = A.f32(D)
            tmp3 = A.f32(512)
            mgb = A.bf16(D)
            mT = A.bf16(8 * 128).rearrange("p (k t) -> p k t", k=8)
            r3 = A.f32(D).rearrange("p (a d) -> p a d", a=1)
            ostg = [A.f32(D) for _ in range(2)]
            ss3 = A.f32(1)
            ms3 = A.f32(1)
            rs3 = A.f32(1)
            ss4 = A.f32(1)
            ms4 = A.f32(1)
            rs4 = A.f32(1)
            ntile3 = 2 * nchunks

            def transpose8(src, dst, skey, dkey, banks):
                for half in range(2):
                    pb = banks[half]
                    pT = ps[pb][:].bitcast(BF16)[:, 0:512].rearrange("p (a t) -> p a t", a=4)
                    for a in range(4):
                        kc = half * 4 + a
                        S.op("pe", lambda e, a=a, kc=kc, pT=pT: e.transpose(out=pT[:, a, :], in_=src[:, kc * 128:(kc + 1) * 128],
                                                                          identity=ident), skey + ["ident"], ["ps%d" % pb])
                    evac(dst[:, half * 4:half * 4 + 4, :], pT, ["ps%d" % pb], [(dkey, half)])

            def p3_front(i):
                b = i % 2
                xk = "x3_%d" % b
                yk = "y3_%d" % b
                zgb = zg2[b]
                S.dma("sp", xk, lambda e: e.dma_start(out=xt3[b][:, 0, :], in_=xown[i * 128:(i + 1) * 128, :]), writes=[xk])
                S.dma("sp", yk, lambda e: e.dma_start(out=yt3[b], in_=YS[i * 128:(i + 1) * 128, :]), writes=[yk])
                rms_tile(xt3[b], 1, xs3, junk3, ss3, ms3, rs3, [xk], "p3")
                transpose8(xs3[:, 0, :], hT3, [("p3xs", 0)], "hT3", (6, 7))
                hkeys = [("hT3", 0), ("hT3", 1)]
                for grp in range(8):
                    pb = grp % 3
                    for kc in range(8):
                        S.op("pe", lambda e, kc=kc, pb=pb, grp=grp: e.matmul(
                            ps[pb][:, :], lhsT=hT3[:, kc, :], rhs=wzg[:, kc, grp * 512:(grp + 1) * 512],
                            start=(kc == 0), stop=(kc == 7)), [("wzg", kc)] + hkeys, ["ps%d" % pb])
                    fn = AF.Silu if grp < 2 else AF.Sigmoid
                    S.op("act", lambda e, pb=pb, grp=grp, fn=fn: e.activation(out=zgb[:, grp * 512:(grp + 1) * 512], in_=ps[pb][:, :],
                                                                           func=fn), ["ps%d" % pb], [("zg", b, grp)])

            def p3_back(i):
                b = i % 2
                xk = "x3_%d" % b
                yk = "y3_%d" % b
                zgb = zg2[b]
                for half in range(2):
                    S.op("dve", lambda e, half=half: e.tensor_tensor(
                        out=u3[:, half * 512:(half + 1) * 512], in0=yt3[b][:, half * 512:(half + 1) * 512],
                        in1=zgb[:, half * 512:(half + 1) * 512], op=ALU.mult), [yk, ("zg", b, half)], [("u3", half)])
                transpose8(u3, uT, [("u3", 0), ("u3", 1)], "uT", (5, 4))
                ukeys = [("uT", 0), ("uT", 1)]
                blks = [(0, 3), (3, 6), (6, 8)]
                for dgi in range(2):
                    for bi, (b0, b1) in enumerate(blks):
                        pb = 3 + (dgi * 3 + bi) % 3
                        for kb in range(b0, b1):
                            S.op("pe", lambda e, kb=kb, pb=pb, dgi=dgi, b0=b0, b1=b1: e.matmul(
                                ps[pb][:, :], lhsT=uT[:, kb, :], rhs=wbr[:, kb, dgi * 512:(dgi + 1) * 512],
                                start=(kb == b0), stop=(kb == b1 - 1)), [("wbr", kb)] + ukeys, ["ps%d" % pb])
                        sg_ap = zgb[:, 1024 + bi * 1024 + dgi * 512:1024 + bi * 1024 + (dgi + 1) * 512]
                        sgk = ("zg", b, 2 + bi * 2 + dgi)
                        mg_ap = merged[:, dgi * 512:(dgi + 1) * 512]
                        if bi == 0:
                            S.op("dve", lambda e, pb=pb, sg_ap=sg_ap, mg_ap=mg_ap: e.tensor_tensor(
                                out=mg_ap, in0=ps[pb][:, :], in1=sg_ap, op=ALU.mult), ["ps%d" % pb, sgk], [("mg", dgi)])
                        else:
                            S.op("dve", lambda e, pb=pb, sg_ap=sg_ap: e.tensor_tensor(
                                out=tmp3, in0=ps[pb][:, :], in1=sg_ap, op=ALU.mult), ["ps%d" % pb, sgk], ["tmp3"])
                            o_ap = mg_ap if bi == 1 else mgb[:, dgi * 512:(dgi + 1) * 512]
                            okey = ("mg", dgi) if bi == 1 else ("mgb", dgi)
                            S.op("dve", lambda e, mg_ap=mg_ap, o_ap=o_ap: e.tensor_tensor(
                                out=o_ap, in0=mg_ap, in1=tmp3, op=ALU.add), ["tmp3", ("mg", dgi)], [okey])
                transpose8(mgb, mT, [("mgb", 0), ("mgb", 1)], "mT", (5, 4))
                mkeys = [("mT", 0), ("mT", 1)]
                for eg in range(2):
                    pb = 3 + eg
                    for kc in range(8):
                        S.op("pe", lambda e, kc=kc, pb=pb, eg=eg: e.matmul(
                            ps[pb][:, :], lhsT=mT[:, kc, :], rhs=wo[:, kc, eg * 512:(eg + 1) * 512],
                            start=(kc == 0), stop=(kc == 7)), [("wo", kc)] + mkeys, ["ps%d" % pb])
                    S.op("dve", lambda e, pb=pb, eg=eg: e.tensor_tensor(
                        out=r3[:, 0, eg * 512:(eg + 1) * 512], in0=ps[pb][:, :], in1=xt3[b][:, 0, eg * 512:(eg + 1) * 512],
                        op=ALU.add), ["ps%d" % pb, xk], [("r3", eg)])
                rms_tile(r3, 1, None, junk4, ss4, ms4, rs4, [("r3", 0), ("r3", 1)], "p3f")
                ok = "ostg%d" % b
                S.op("dve", lambda e: e.scalar_tensor_tensor(out=ostg[b], in0=r3[:, 0, :], scalar=rs4[:, 0:1], in1=fg,
                                                            op0=ALU.mult, op1=ALU.mult),
                     [("r3", 0), ("r3", 1), "p3frstd", "fg"], [ok])
                S.dma("pool", ok, lambda e: e.dma_start(out=out[i * 128:(i + 1) * 128, :], in_=ostg[b]),
                      reads=[ok], writes=[("out", i)])

            p3_front(0)
            for i in range(ntile3):
                if i + 1 < ntile3:
                    p3_front(i + 1)
                p3_back(i)
            A.top = m3
        S.barrier()
        with nc.Block() as block:
            S.emit(block)
    return nc


def np_bucket(dist):
    n = np.maximum(dist, 0)
    nf = np.maximum(n, 1).astype(np.float32)
    large = 16 + (np.log(nf / np.float32(16)) / np.float32(math.log(128 / 16)) * np.float32(16)).astype(np.int32)
    return np.where(n < 16, n, np.minimum(large, 31))


def own_rows(h):
    return np.concatenate([np.arange(256 * (2 * m + h), 256 * (2 * m + h) + 256) for m in range(NCH)])


def make_in_maps(x, mem, norm_gain, w_in, rel_bias, mem_norm_gain, w_mem_kv, w_branch, w_out, final_norm_gain):
    f = np.float32
    w = np.asarray(w_in[0], f)

    def cols(*names):
        return np.ascontiguousarray(np.concatenate([w[:, OFF[n][0]:OFF[n][1]] for n in names], axis=1))

    shared = {
        "gain": np.ascontiguousarray(np.asarray(norm_gain[0], f).reshape(8, 128).T),
        "mgain": np.ascontiguousarray(np.asarray(mem_norm_gain[0], f).reshape(8, 128).T),
        "fgain": np.ascontiguousarray(np.broadcast_to(np.asarray(final_norm_gain, f)[None, :], (128, D))),
        "w_kvT": cols("ka", "kb", "ik"),
        "w_v": cols("va", "vb"),
        "w_qT": cols("qa", "qb", "iq", "qm"),
        "w_iw": cols("iw"),
        "w_zg": cols("za", "zb", "zm", "ga", "gb", "gm"),
        "w_mkv": np.ascontiguousarray(np.asarray(w_mem_kv[0], f)),
        "w_br": np.ascontiguousarray(np.asarray(w_branch[0], f)),
        "w_o": np.ascontiguousarray(np.asarray(w_out[0], f)),
        "b31": np.ascontiguousarray(np.broadcast_to(np.asarray(rel_bias, f)[31][None, :], (128, 12))),
        "ident": np.eye(128, dtype=f).astype(ml_dtypes.bfloat16),
    }
    E = np.zeros((32, 32, 128), f)
    for n in range(32):
        E[n, n, :] = 1.0
    shared["E"] = E.reshape(32, 32 * 128).astype(ml_dtypes.bfloat16)
    rb = np.asarray(rel_bias, f)
    sl = np.arange(128)[:, None]
    tl = np.arange(256)[None, :]
    maps = []
    for c in range(8):
        b, h = c // 2, c % 2
        rows = own_rows(h)
        m = dict(shared)
        m["xall"] = np.ascontiguousarray(np.asarray(x[b], f))
        m["xown"] = np.ascontiguousarray(np.asarray(x[b], f)[rows])
        m["memx"] = np.ascontiguousarray(np.asarray(mem[b], f))
        tabraw = np.zeros((12, 128, 5, 256), f)
        cmt = np.zeros((128, 5, 256), f)
        for ii, i in enumerate(range(1, 6)):
            Di = 256 * h + 256 - 128 * i
            d = Di + tl - sl
            bk = np_bucket(d)
            tabraw[:, :, ii, :] = rb[bk].transpose(2, 0, 1)
            cmt[:, ii, :] = np.where(d < 0, NEG, 0.0)
        m["tabraw"] = tabraw
        m["cmt"] = cmt
        cm = np.zeros((128, 2, 512), f)
        k = np.arange(512)[None, :]
        for slot in range(2):
            lim = 256 * h + 128 * slot + np.arange(128)[:, None]
            cm[:, slot, :] = np.where(k <= lim, 0.0, NEG)
        m["cmask"] = cm.astype(ml_dtypes.bfloat16)
        pn = np.zeros((128, 16, 32), f)
        no = np.full((128, 16, 32), NEG, f)
        for mm in range(16):
            own = 2 * mm + h
            pn[:, mm, own:] = NEG
            no[:, mm, own] = 0.0
        m["pastneg"] = pn.astype(ml_dtypes.bfloat16)
        m["notown"] = no.astype(ml_dtypes.bfloat16)
        maps.append(m)
    return maps


_NC_CACHE = {}


def kernel(x, mem, norm_gain, w_in, rel_bias, mem_norm_gain, w_mem_kv, w_branch, w_out, final_norm_gain):
    maps = make_in_maps(x, mem, norm_gain, w_in, rel_bias, mem_norm_gain, w_mem_kv, w_branch, w_out, final_norm_gain)
    if "nc" not in _NC_CACHE:
        _NC_CACHE["nc"] = build_program()
    res = run_bass_kernel_spmd(_NC_CACHE["nc"], maps, core_ids=list(range(8)))
    outf = np.zeros((4, T, D), np.float32)
    for c in range(8):
        b, h = c // 2, c % 2
        outf[b, own_rows(h)] = res.results[c]["out"]
    return outf
```

```python
import math
from contextlib import ExitStack

import numpy as np
import ml_dtypes

import concourse.bass as bass
import concourse.mybir as mybir
from concourse.bass_utils import run_bass_kernel_spmd

F32 = mybir.dt.float32
BF16 = mybir.dt.bfloat16
AF = mybir.ActivationFunctionType
ALU = mybir.AluOpType
AX = mybir.AxisListType

T = 8192
D = 1024
NOWN = 4096
NCH = 16
NEG = -30000.0
EPS = 1e-6
NBIS = 14
ARENA = 53200

OFF = {}
_o = 0
for _n, _s in [("qa", 384), ("ka", 384), ("va", 384), ("za", 384), ("qb", 384), ("kb", 384), ("vb", 384),
               ("zb", 384), ("iq", 512), ("ik", 64), ("iw", 8), ("qm", 256), ("zm", 256), ("ga", 1024),
               ("gb", 1024), ("gm", 1024)]:
    OFF[_n] = (_o, _o + _s)
    _o += _s
assert _o == 7240


class Sched:
    ENG = ("pe", "act", "dve", "pool", "sp")
    HND = {"pe": "tensor", "act": "scalar", "dve": "vector", "pool": "gpsimd", "sp": "sync"}

    def __init__(self, nc, stack):
        self.nc = nc
        self.stack = stack
        self.q = {e: [] for e in self.ENG}
        self.cnt = {e: 0 for e in self.ENG}
        self.semobj = {}
        for e in self.ENG:
            self.semobj["prog_" + e] = stack.enter_context(nc.semaphore("prog_" + e))
        self.dcount = {}
        self.lastw = {}
        self.readers = {}
        self.waited = {e: {} for e in self.ENG}

    def _deps(self, reads, writes):
        deps = []
        for k in reads:
            d = self.lastw.get(k)
            if d is not None:
                deps.append(d)
        for k in writes:
            d = self.lastw.get(k)
            if d is not None:
                deps.append(d)
            deps.extend(self.readers.get(k, ()))
        return deps

    def _commit(self, dep, reads, writes):
        for k in writes:
            self.lastw[k] = dep
            self.readers[k] = []
        for k in reads:
            self.readers.setdefault(k, []).append(dep)

    def _waits(self, eng, deps):
        need = {}
        for (sk, val, deng) in deps:
            if deng == eng and eng == "pe":
                continue
            if self.waited[eng].get(sk, 0) >= val:
                continue
            if need.get(sk, 0) < val:
                need[sk] = val
        for sk, val in need.items():
            self.waited[eng][sk] = val
        return list(need.items())

    def op(self, eng, fn, reads=(), writes=()):
        writes = list(writes) + [k for k in reads if isinstance(k, str) and k.startswith("ps") and k not in writes]
        waits = self._waits(eng, self._deps(reads, writes))
        self.cnt[eng] += 1
        sk = "prog_" + eng
        self.q[eng].append((waits, fn, (sk, 1)))
        dep = (sk, self.cnt[eng], eng)
        self._commit(dep, reads, writes)
        return dep

    def dma(self, eng, slot, fn, reads=(), writes=()):
        sk = "d_" + slot
        if sk not in self.semobj:
            self.semobj[sk] = self.stack.enter_context(self.nc.semaphore(sk))
            self.dcount[sk] = 0
        waits = self._waits(eng, self._deps(reads, writes))
        self.dcount[sk] += 16
        self.q[eng].append((waits, fn, (sk, 16)))
        dep = (sk, self.dcount[sk], "dma")
        self._commit(dep, reads, writes)
        return dep

    def barrier(self):
        deps = [("prog_" + e, self.cnt[e], e + "_b") for e in self.ENG if self.cnt[e] > 0]
        deps += [(sk, v, "dma") for sk, v in self.dcount.items() if v > 0]
        for e in self.ENG:
            waits = self._waits(e, [d for d in deps if d[0] != "prog_" + e])
            if waits:
                self.q[e].append((waits, None, None))

    def final_wait(self, eng, keys):
        deps = [self.lastw[k] for k in keys if k in self.lastw]
        waits = self._waits(eng, deps)
        self.q[eng].append((waits, None, None))

    def emit(self, block):
        def mk(ename):
            items = self.q[ename]

            def body(e):
                for waits, fn, inc in items:
                    for sk, val in waits:
                        e.wait_ge(self.semobj[sk], val)
                    if fn is not None:
                        fn(e).then_inc(self.semobj[inc[0]], inc[1])
            return body
        for ename in self.ENG:
            getattr(block, self.HND[ename])(mk(ename))


class Arena:
    def __init__(self, t, ncols):
        self.t = t
        self.n = ncols
        self.top = 0

    def f32(self, cols, parts=128):
        cols_al = (cols + 15) // 16 * 16
        off = self.top
        self.top += cols_al
        assert self.top <= self.n, ("arena overflow", self.top)
        return self.t[0:parts, off:off + cols]

    def bf16(self, cols, parts=128):
        c32 = ((cols + 1) // 2 + 15) // 16 * 16
        off = self.top
        self.top += c32
        assert self.top <= self.n, ("arena overflow", self.top)
        return self.t[0:parts, off:off + c32].bitcast(BF16)[:, 0:cols]


def build_program(dbg=False, nchunks=NCH, phases=("p1", "p2", "p3"), nt1a=16, nt1b=8):
    nc = bass.Bass("TRN2", target_bir_lowering=False)

    def din(name, shape, dt=F32):
        return nc.dram_tensor(name, list(shape), dt, kind="ExternalInput").ap()

    skind = "ExternalOutput" if dbg else "Internal"

    def dscr(name, shape, dt):
        return nc.dram_tensor(name, list(shape), dt, kind=skind).ap()

    xall = din("xall", [T, D])
    xown = din("xown", [NOWN, D])
    memx = din("memx", [256, D])
    gain = din("gain", [128, 8])
    mgain = din("mgain", [128, 8])
    fgain = din("fgain", [128, D])
    w_kvT = din("w_kvT", [D, 832])
    w_v = din("w_v", [D, 768])
    w_qT = din("w_qT", [D, 1536])
    w_iw = din("w_iw", [D, 8])
    w_zg = din("w_zg", [D, 4096])
    w_mkv = din("w_mkv", [D, 512])
    w_br = din("w_br", [D, D])
    w_o = din("w_o", [D, D])
    tabraw = din("tabraw", [12, 128, 5, 256])
    cmt = din("cmt", [128, 5, 256])
    b31 = din("b31", [128, 12])
    cmask = din("cmask", [128, 2, 512], BF16)
    pastneg = din("pastneg", [128, 16, 32], BF16)
    notown = din("notown", [128, 16, 32], BF16)
    identd = din("ident", [128, 128], BF16)
    Ed = din("E", [32, 32 * 128], BF16)

    KT = dscr("KT", [832, T], BF16)
    VS = dscr("VS", [12, 128, 64 * 65], BF16)
    QT = dscr("QT", [1536, NOWN], BF16)
    YS = dscr("YS", [NOWN, D], F32)
    out = nc.dram_tensor("out", [NOWN, D], F32, kind="ExternalOutput").ap()

    with ExitStack() as st:
        S = Sched(nc, st)
        arena_t = st.enter_context(nc.sbuf_tensor("arena", [128, ARENA], F32))
        ps = [st.enter_context(nc.psum_tensor("ps%d" % i, [128, 512], F32)) for i in range(8)]
        A = Arena(arena_t, ARENA)

        ident = A.bf16(128)
        gain_sb = A.f32(8)
        mgain_sb = A.f32(8)
        iw_all = A.f32(32 * 8)
        kms = A.f32(3 * 32)
        S.dma("sp", "c0", lambda e: e.dma_start(out=ident, in_=identd), writes=["ident"])
        S.dma("sp", "c1", lambda e: e.dma_start(out=gain_sb, in_=gain), writes=["gain"])
        S.dma("sp", "c2", lambda e: e.dma_start(out=mgain_sb, in_=mgain), writes=["mgain"])
        S.op("pool", lambda e: e.memset(kms, 0.0), [], ["kms0"])
        base_top = A.top

        evac_rr = [0]

        def evac(out_ap, in_ap, reads, writes, scale=None, eng=None):
            if eng is None:
                eng = ("act", "dve")[evac_rr[0] % 2]
                evac_rr[0] += 1
            if eng == "act":
                if scale is None:
                    S.op("act", lambda e: e.activation(out=out_ap, in_=in_ap, func=AF.Copy), reads, writes)
                else:
                    S.op("act", lambda e: e.activation(out=out_ap, in_=in_ap, func=AF.Copy, scale=float(scale)),
                         reads, writes)
            else:
                if scale is None:
                    S.op("dve", lambda e: e.tensor_copy(out=out_ap, in_=in_ap), reads, writes)
                else:
                    S.op("dve", lambda e: e.tensor_scalar(out=out_ap, in0=in_ap, scalar1=float(scale), scalar2=None,
                                                          op0=ALU.mult), reads, writes)

        def load_weight(wd, ncols, dst, gain_ap, tag, stg):
            for kc in range(8):
                sb = stg[kc % 2]
                sk = "wstg%d" % (kc % 2)
                S.dma("sp", sk, lambda e, kc=kc, sb=sb: e.dma_start(out=sb[:, 0:ncols], in_=wd[kc * 128:(kc + 1) * 128, :]),
                      writes=[sk])
                if gain_ap is None:
                    evac(dst[:, kc, :], sb[:, 0:ncols], [sk], [(tag, kc)])
                else:
                    g = gain_ap[:, kc:kc + 1]
                    if kc % 2 == 0:
                        S.op("act", lambda e, kc=kc, sb=sb, g=g: e.activation(out=dst[:, kc, :], in_=sb[:, 0:ncols],
                                                                           func=AF.Copy, scale=g),
                             [sk, "gain", "mgain"], [(tag, kc)])
                    else:
                        S.op("dve", lambda e, kc=kc, sb=sb, g=g: e.tensor_scalar(out=dst[:, kc, :], in0=sb[:, 0:ncols],
                                                                              scalar1=g, scalar2=None, op0=ALU.mult),
                             [sk, "gain", "mgain"], [(tag, kc)])

        def rms_tile(x_ap, nsub, xs_out, junk, ss, ms, rstd, rkeys, tag):
            for a in range(nsub):
                S.op("act", lambda e, a=a: e.activation(out=junk, in_=x_ap[:, a, :], func=AF.Square,
                                                       accum_out=ss[:, a:a + 1]),
                     rkeys, [tag + "junk", (tag + "ss", a)])
            S.op("dve", lambda e: e.tensor_scalar(out=ms[:, 0:nsub], in0=ss[:, 0:nsub], scalar1=1.0 / D, scalar2=EPS,
                                                  op0=ALU.mult, op1=ALU.add),
                 [(tag + "ss", a) for a in range(nsub)], [tag + "ms"])
            S.op("act", lambda e: e.activation(out=ms[:, 0:nsub], in_=ms[:, 0:nsub], func=AF.Sqrt),
                 [tag + "ms"], [tag + "ms"])
            S.op("dve", lambda e: e.reciprocal(out=rstd[:, 0:nsub], in_=ms[:, 0:nsub]), [tag + "ms"], [tag + "rstd"])
            if xs_out is not None:
                for a in range(nsub):
                    S.op("dve", lambda e, a=a: e.tensor_scalar(out=xs_out[:, a, :], in0=x_ap[:, a, :],
                                                              scalar1=rstd[:, a:a + 1], scalar2=None, op0=ALU.mult),
                         rkeys + [tag + "rstd"], [(tag + "xs", a)])

        def proj_phase(xsrc, ntiles, wT, featT, wtok, tok_fn, tag):
            m0 = A.top
            xt = [A.f32(4 * D).rearrange("p (a d) -> p a d", a=4) for _ in range(2)]
            xs = A.bf16(4 * D).rearrange("p (a d) -> p a d", a=4)
            junk = A.bf16(D)
            hT = [A.bf16(8 * 512).rearrange("p (k t) -> p k t", k=8) for _ in range(2)]
            stg = [A.bf16(512) for _ in range(3)]
            ss = A.f32(4)
            ms = A.f32(4)
            rstd = A.f32(4)
            fb = 0
            for i in range(ntiles):
                b = i % 2
                xk = tag + "xt%d" % b
                S.dma("sp", xk, lambda e, i=i, b=b: e.dma_start(
                    out=xt[b], in_=xsrc[i * 512:(i + 1) * 512, :].rearrange("(a p) d -> p a d", p=128)), writes=[xk])
                rms_tile(xt[b], 4, xs, junk, ss, ms, rstd, [xk], tag)
                hk = tag + "hT%d" % b
                for kc in range(8):
                    pb = 6 + (kc % 2)
                    pT = ps[pb][:].bitcast(BF16)[:, 0:512].rearrange("p (a t) -> p a t", a=4)
                    for a in range(4):
                        S.op("pe", lambda e, a=a, kc=kc, pT=pT: e.transpose(out=pT[:, a, :],
                                                                          in_=xs[:, a, kc * 128:(kc + 1) * 128],
                                                                          identity=ident),
                             [(tag + "xs", a), "ident"], ["ps%d" % pb])
                    evac(hT[b][:, kc, :], ps[pb][:].bitcast(BF16)[:, 0:512], ["ps%d" % pb], [(hk, kc)])
                hkeys = [(hk, kc) for kc in range(8)]
                for (c0, ncol, r0, scale, kblk, dst) in featT:
                    pb = fb % 3
                    fb += 1
                    pk = "ps%d" % pb
                    for kc in range(8):
                        S.op("pe", lambda e, kc=kc, pb=pb, c0=c0, ncol=ncol, b=b: e.matmul(
                            ps[pb][0:ncol, :], lhsT=wT[:, kc, c0:c0 + ncol], rhs=hT[b][:, kc, :],
                            start=(kc == 0), stop=(kc == 7)),
                             [(tag + "wT", kc), (hk, kc)], [pk])
                    sb = stg[fb % 3]
                    sk = tag + "stg%d" % (fb % 3)
                    evac(sb[0:ncol, :], ps[pb][0:ncol, :], [pk], [sk], scale=scale)
                    if kblk is not None:
                        S.op("dve", lambda e, pb=pb, kblk=kblk, i=i: e.tensor_reduce(
                            out=kms[:, kblk * 32 + 2 * i: kblk * 32 + 2 * i + 2],
                            in_=ps[pb][:, :].rearrange("p (n s) -> p n s", s=256), axis=AX.X, op=ALU.add),
                             [pk, "kms0"], [("kms", kblk, i), pk])
                    S.dma("pool", sk, lambda e, sb=sb, r0=r0, ncol=ncol, i=i, dst=dst: e.dma_start(
                        out=dst[r0:r0 + ncol, i * 512:(i + 1) * 512], in_=sb[0:ncol, :]), reads=[sk], writes=[(tag + "featdst", r0, i)])
                for a in range(4):
                    tok_fn(i, a, b, hk, hT[b])
            A.top = m0

        if "p1" in phases:
            m1 = A.top
            wkvT = A.bf16(8 * 832).rearrange("p (k n) -> p k n", k=8)
            wv = A.bf16(8 * 768).rearrange("p (k n) -> p k n", k=8)
            wstg = [A.f32(1536) for _ in range(2)]
            load_weight(w_kvT, 832, wkvT, gain_sb, "p1awT", wstg)
            load_weight(w_v, 768, wv, gain_sb, "p1awv", wstg)
            vstg = [A.bf16(4 * 12 * 65).rearrange("p (h a d) -> p h a d", a=4, h=12) for _ in range(2)]
            for vb_ in range(2):
                S.op("pool", lambda e, vb_=vb_: e.memset(vstg[vb_], 1.0), [], ["vstg%d" % vb_])
            featT = [(blk * 128, 128, blk * 128, None, (blk if blk < 3 else None), KT) for blk in range(6)]
            featT.append((768, 64, 768, None, None, KT))

            def tok_v(i, a, b, hk, hTb):
                vb_ = i % 2
                vk = "vstg%d" % vb_
                for g, (c0, ncol) in enumerate([(0, 512), (512, 256)]):
                    pb = 3 + (a * 2 + g) % 3
                    pk = "ps%d" % pb
                    for kc in range(8):
                        S.op("pe", lambda e, kc=kc, pb=pb, c0=c0, ncol=ncol, a=a: e.matmul(
                            ps[pb][:, 0:ncol], lhsT=hTb[:, kc, a * 128:(a + 1) * 128], rhs=wv[:, kc, c0:c0 + ncol],
                            start=(kc == 0), stop=(kc == 7)),
                             [("p1awv", kc), (hk, kc)], [pk])
                    h0 = c0 // 64
                    nh = ncol // 64
                    evac(vstg[vb_][:, h0:h0 + nh, a, 0:64], ps[pb][:, 0:ncol].rearrange("p (h d) -> p h d", d=64),
                         [pk, vk], [(vk, a, g)])
                if a == 3:
                    S.dma("pool", vk, lambda e, i=i, vb_=vb_: e.dma_start(
                        out=VS[:, :, i * 4 * 65:(i + 1) * 4 * 65].rearrange("h p x -> p h x"),
                        in_=vstg[vb_].rearrange("p h a d -> p h (a d)")),
                        reads=[(vk, aa, g) for aa in range(4) for g in range(2)], writes=[("VS", i)])

            proj_phase(xall, nt1a, wkvT, featT, wv, tok_v, "p1a")
            A.top = m1
            S.barrier()

            wqT = A.bf16(8 * 1536).rearrange("p (k n) -> p k n", k=8)
            wiw = A.bf16(8 * 8).rearrange("p (k n) -> p k n", k=8)
            wstg = [A.f32(1536) for _ in range(2)]
            load_weight(w_qT, 1536, wqT, gain_sb, "p1bwT", wstg)
            load_weight(w_iw, 8, wiw, gain_sb, "p1bwiw", wstg)
            featT = []
            for blk in range(12):
                scale = None if 6 <= blk < 10 else 0.125
                featT.append((blk * 128, 128, blk * 128, scale, None, QT))

            def tok_iw(i, a, b, hk, hTb):
                pb = 3 + a % 3
                pk = "ps%d" % pb
                for kc in range(8):
                    S.op("pe", lambda e, kc=kc, pb=pb, a=a: e.matmul(
                        ps[pb][:, 0:8], lhsT=hTb[:, kc, a * 128:(a + 1) * 128], rhs=wiw[:, kc, :],
                        start=(kc == 0), stop=(kc == 7)),
                         [("p1bwiw", kc), (hk, kc)], [pk])
                sl = 4 * i + a
                evac(iw_all[:, sl * 8:(sl + 1) * 8], ps[pb][:, 0:8], [pk], [("iw", sl)])

            proj_phase(xown, nt1b, wqT, featT, wiw, tok_iw, "p1b")
            A.top = m1
            S.barrier()


        if "p2" in phases:
            m2 = A.top
            ikT = A.bf16(T)
            tabs = A.bf16(12 * 5 * 256).rearrange("p (h i t) -> p h i t", h=12, i=5)
            Esb = A.bf16(32 * 128).rearrange("p (n s) -> p n s", n=32)
            b31_sb = A.f32(12)
            cmask_sb = A.bf16(2 * 512).rearrange("p (s k) -> p s k", s=2)
            pastneg_sb = A.bf16(16 * 32).rearrange("p (m n) -> p m n", m=16)
            notown_sb = A.bf16(16 * 32).rearrange("p (m n) -> p m n", m=16)
            kmbf = A.bf16(3 * 32).rearrange("p (b n) -> p b n", b=3)
            mbT = A.bf16(64 * 256).rearrange("p (j t) -> p j t", j=64)
            mbmT = A.bf16(6 * 256).rearrange("p (h t) -> p h t", h=6)
            iq_sb = A.bf16(8 * 256).rearrange("p (h t) -> p h t", h=8)
            qz = [A.bf16(nb_ * 512).rearrange("p (b t) -> p b t", b=nb_) for nb_ in (3, 3, 2)]
            gsb = A.f32(2 * 192).rearrange("p (a n) -> p a n", a=2)
            praw = [gsb.rearrange("p a n -> p (a n)")[:, 0:260]]
            ystg = [A.f32(2 * 384).rearrange("p (s d) -> p s d", s=2) for _ in range(2)]
            mkT = A.bf16(2 * 256).rearrange("p (b t) -> p b t", b=2)
            memV = A.bf16(2 * 4 * 65).rearrange("p (j h d) -> p j h d", j=2, h=4)
            smalls = A.f32(48)
            rmax = smalls[:, 0:1]
            lo = smalls[:, 1:2]
            w0 = smalls[:, 2:3]
            mid = smalls[:, 3:4]
            cnt = smalls[:, 4:5]
            step = smalls[:, 5:6]
            gm = A.f32(6 * 32).rearrange("p (h n) -> p h n", h=6)
            mx8 = A.f32(6 * 8).rearrange("p (h n) -> p h n", h=6)
            mbm = A.bf16(2 * 6 * 32).rearrange("p (s h n) -> p s h n", s=2, h=6)
            rden = smalls[:, 8:12]
            bmax = smalls[:, 16:32]
            bmin = smalls[:, 32:48]
            p16 = [A.bf16(512) for _ in range(7)]
            m_sc = A.top
            sc = A.f32(T)
            mb = A.bf16(T)
            dgw = A.bf16(8 * 128).rearrange("p (h t) -> p h t", h=8)
            khT = A.bf16(T)
            vh = [A.bf16(64 * 65) for _ in range(2)]
            mB = A.top
            A.top = m_sc

            S.op("pool", lambda e: e.memset(ikT, 0.0), [], ["ikT"])
            S.op("pool", lambda e: e.memset(Esb, 0.0), [], ["E"])
            S.op("pool", lambda e: e.memset(mbmT, 0.0), [], [("mbmT", 0), ("mbmT", 1)])
            S.op("pool", lambda e: e.memset(iq_sb, 0.0), [], ["iq"])
            S.dma("sp", "s_ik", lambda e: e.dma_start(out=ikT[0:64, 0:512 * nt1a], in_=KT[768:832, 0:512 * nt1a]), writes=["ikT"])
            S.dma("sp", "s_E", lambda e: e.dma_start(out=Esb[0:32], in_=Ed.rearrange("p (n s) -> p n s", n=32)), writes=["E"])
            S.dma("sp", "s_b31", lambda e: e.dma_start(out=b31_sb, in_=b31), writes=["b31"])
            S.dma("sp", "s_cm", lambda e: e.dma_start(out=cmask_sb, in_=cmask), writes=["cmask"])
            S.dma("sp", "s_pn", lambda e: e.dma_start(out=pastneg_sb, in_=pastneg), writes=["pastneg"])
            S.dma("sp", "s_no", lambda e: e.dma_start(out=notown_sb, in_=notown), writes=["notown"])
            cmt_sb = A.f32(5 * 256).rearrange("p (i t) -> p i t", i=5)
            tstg = [A.f32(5 * 256).rearrange("p (i t) -> p i t", i=5) for _ in range(2)]
            S.dma("sp", "s_cmt", lambda e: e.dma_start(out=cmt_sb, in_=cmt), writes=["cmt"])
            for hd in range(12):
                tk = "tstg%d" % (hd % 2)
                S.dma("sp", tk, lambda e, hd=hd: e.dma_start(out=tstg[hd % 2], in_=tabraw[hd]), writes=[tk])
                S.op("dve", lambda e, hd=hd: e.scalar_tensor_tensor(out=tabs[:, hd], in0=tstg[hd % 2],
                                                                   scalar=b31_sb[:, hd:hd + 1], in1=cmt_sb,
                                                                   op0=ALU.subtract, op1=ALU.add),
                     [tk, "b31", "cmt"], [("tabs", hd)])
            S.op("dve", lambda e: e.tensor_copy(out=kmbf, in_=kms.rearrange("p (b n) -> p b n", b=3)), [], ["kmbf"])
            for qi in range(3):
                S.op("pool", lambda e, qi=qi: e.memset(qz[qi], 0.0), [], ["qzero%d" % qi])
            xm = A.f32(2 * D).rearrange("p (a d) -> p a d", a=2)
            xsm = A.bf16(2 * D).rearrange("p (a d) -> p a d", a=2)
            junkm = A.bf16(D)
            hTm = A.bf16(8 * 256).rearrange("p (k t) -> p k t", k=8)
            wmk = A.bf16(8 * 512).rearrange("p (k n) -> p k n", k=8)
            ssm = A.f32(2)
            msm = A.f32(2)
            rsm = A.f32(2)
            wstg = [A.f32(512) for _ in range(2)]
            load_weight(w_mkv, 512, wmk, mgain_sb, "wmk", wstg)
            S.dma("sp", "s_xm", lambda e: e.dma_start(out=xm, in_=memx.rearrange("(a p) d -> p a d", p=128)), writes=["xm"])
            rms_tile(xm, 2, xsm, junkm, ssm, msm, rsm, ["xm"], "mem")
            for kc in range(8):
                pb = 3 + kc % 2
                pT = ps[pb][:].bitcast(BF16)[:, 0:256].rearrange("p (a t) -> p a t", a=2)
                for a in range(2):
                    S.op("pe", lambda e, a=a, kc=kc, pT=pT: e.transpose(out=pT[:, a, :], in_=xsm[:, a, kc * 128:(kc + 1) * 128],
                                                                      identity=ident),
                         [("memxs", a), "ident"], ["ps%d" % pb])
                evac(hTm[:, kc, :], ps[pb][:].bitcast(BF16)[:, 0:256], ["ps%d" % pb], [("hTm", kc)])
            for blk in range(2):
                pb = blk
                for kc in range(8):
                    S.op("pe", lambda e, kc=kc, pb=pb, blk=blk: e.matmul(ps[pb][:, 0:256], lhsT=wmk[:, kc, blk * 128:(blk + 1) * 128],
                                                                       rhs=hTm[:, kc, :], start=(kc == 0), stop=(kc == 7)),
                         [("wmk", kc), ("hTm", kc)], ["ps%d" % pb])
                evac(mkT[:, blk, :], ps[pb][:, 0:256], ["ps%d" % pb], [("mkT", blk)])
            S.op("pool", lambda e: e.memset(memV, 1.0), [], ["memV"])
            for j in range(2):
                pb = 5 + j
                for kc in range(8):
                    S.op("pe", lambda e, kc=kc, pb=pb, j=j: e.matmul(ps[pb][:, 0:256], lhsT=hTm[:, kc, j * 128:(j + 1) * 128],
                                                                   rhs=wmk[:, kc, 256:512], start=(kc == 0), stop=(kc == 7)),
                         [("wmk", kc), ("hTm", kc)], ["ps%d" % pb])
                evac(memV[:, j, :, 0:64], ps[pb][:, 0:256].rearrange("p (h d) -> p h d", d=64), ["ps%d" % pb, "memV"],
                     [("memV", j)])
            S.barrier()
            A.top = mB

            cnts = {"ps": 0, "ips": 0, "p16": 0, "r16": 0, "acc": 0, "y": 0}
            deferred = []

            def flush_deferred(keep=0):
                while len(deferred) > keep:
                    for f in deferred.pop(0):
                        f()

            ALLBR = (("dsa", 3, 384, 384, 6, 6, 384, 0), ("moba", 3, 0, 0, 0, 0, 0, 1), ("mem", 2, 1280, 0, 0, 0, 768, 2))

            def load_q(mc, br):
                (mode, npair, qrow0, krow0, vidx0, tab0, ycol0, qi) = br
                qk = "qz%d" % qi
                qsrc = QT[qrow0:qrow0 + 128 * npair, 256 * mc:256 * mc + 256].rearrange("(b two p) t -> two p b t", two=2, p=64)
                S.dma("sp", qk + "a", lambda e: e.dma_start(out=qz[qi][0:64, 0:npair, 0:256], in_=qsrc[0]),
                      reads=["qzero%d" % qi], writes=[(qk, 0)])
                S.dma("sp", qk + "b", lambda e: e.dma_start(out=qz[qi][64:128, 0:npair, 256:512], in_=qsrc[1]),
                      reads=["qzero%d" % qi], writes=[(qk, 1)])

            def gate_part1(m):
                load_q(m, ALLBR[1])
                mqk = [("qz1", 0), ("qz1", 1)]
                for slot in range(2):
                    for hd in range(6):
                        p0 = (hd % 2) * 64
                        c0 = (hd % 2) * 256 + slot * 128
                        gb_ = 7 if hd % 2 == 0 else 4
                        gc = (slot * 3 + hd // 2) * 32
                        S.op("pe", lambda e, hd=hd, p0=p0, c0=c0, gb_=gb_, gc=gc: e.matmul(
                            ps[gb_][:, gc:gc + 32], lhsT=qz[1][p0:p0 + 64, hd // 2, c0:c0 + 128],
                            rhs=kmbf[p0:p0 + 64, hd // 2, :], start=True, stop=True), mqk + ["kmbf"], ["ps%d" % gb_])
                for par, gb_ in ((0, 7), (1, 4)):
                    S.op("act", lambda e, par=par, gb_=gb_: e.activation(out=gsb[:, par, :], in_=ps[gb_][:, 0:192], func=AF.Copy),
                         ["ps%d" % gb_], ["gp"])
                for slot in range(2):
                    for hd in range(6):
                        gc = (slot * 3 + hd // 2) * 32
                        S.op("dve", lambda e, hd=hd, gc=gc: e.tensor_tensor(
                            out=gm[:, hd, :], in0=gsb[:, hd % 2, gc:gc + 32], in1=pastneg_sb[:, m, :], op=ALU.add),
                             ["gp", "pastneg"], [("gm", hd)])
                        S.op("dve", lambda e, hd=hd: e.max(out=mx8[:, hd, :], in_=gm[:, hd, :]), [("gm", hd)], [("mx8", hd)])
                        S.op("dve", lambda e, hd=hd, slot=slot: e.scalar_tensor_tensor(
                            out=mbm[:, slot, hd, :], in0=gm[:, hd, :], scalar=mx8[:, hd, 2:3], in1=notown_sb[:, m, :],
                            op0=ALU.is_lt, op1=ALU.mult), [("gm", hd), ("mx8", hd), "notown"], [("mbm", slot, hd)])

            def gate_part2():
                if True:
                    for slot in range(2):
                        pTm = ps[7][:].bitcast(BF16)[0:32, 0:6 * 128].rearrange("p (h t) -> p h t", h=6)
                        for hd in range(6):
                            S.op("pe", lambda e, hd=hd, slot=slot, pTm=pTm: e.transpose(out=pTm[:, hd, :], in_=mbm[:, slot, hd, :],
                                                                                     identity=ident),
                                 [("mbm", slot, hd), "ident"], ["ps7"])
                        evac(mbmT[0:32, :, slot * 128:(slot + 1) * 128], pTm, ["ps7"], [("mbmT", slot)], eng="act")

            def gen_attention(m, modes, gate_for=None):
                L = 512 * (m + 1)
                nj = 4 * (m + 1)
                t0 = 256 * m
                pob = 5
                pok = "ps5"
                branches = [br for br in ALLBR if br[0] in modes]
                for br in branches:
                    if br[0] != "moba":
                        load_q(m, br)
                if gate_for is not None:
                    gate_part1(gate_for)
                first_pair = True
                for (mode, npair, qrow0, krow0, vidx0, tab0, ycol0, qi) in branches:
                    nh = 2 * npair
                    qkeys = [("qz%d" % qi, 0), ("qz%d" % qi, 1)]
                    yb_ = cnts["y"] % 2
                    cnts["y"] += 1
                    njj = 2 if mode == "mem" else nj
                    nseg = 1 if njj < 4 else 4
                    jps = njj // nseg
                    for b in range(npair):
                        if gate_for is not None and b == 1 and first_pair:
                            gate_part2()
                            first_pair = False
                        if mode != "mem":
                            flush_deferred()
                            for sg_ in range(nseg):
                                c0, c1 = sg_ * jps * 128, (sg_ + 1) * jps * 128
                                S.dma("sp", "khTs%d" % sg_, lambda e, b=b, c0=c0, c1=c1, krow0=krow0: e.dma_start(
                                    out=khT[:, c0:c1], in_=KT[krow0 + 128 * b:krow0 + 128 * b + 128, c0:c1]),
                                    writes=[("khT", j_) for j_ in range(sg_ * jps, (sg_ + 1) * jps)])
                                for wh in range(2):
                                    v0, v1 = sg_ * jps * 65, (sg_ + 1) * jps * 65
                                    S.dma("sp", "vhs%d_%d" % (wh, sg_), lambda e, b=b, wh=wh, v0=v0, v1=v1, vidx0=vidx0: e.dma_start(
                                        out=vh[wh][:, v0:v1], in_=VS[vidx0 + 2 * b + wh, :, v0:v1]),
                                        writes=[("vh", wh, j_) for j_ in range(sg_ * jps, (sg_ + 1) * jps)])
                        q_ap = qz[qi][:, b, :]
                        for j in range(njj):
                            sb_ = cnts["ips"] % 4
                            cnts["ips"] += 1
                            sk = "ps%d" % sb_
                            pS = ps[sb_][:, :]
                            has_tab = (mode != "mem") and (j >= nj - 5)
                            if mode == "mem":
                                S.op("pe", lambda e, b=b, j=j, pS=pS, q_ap=q_ap: e.matmul(
                                    pS, lhsT=mkT[:, b, j * 128:(j + 1) * 128], rhs=q_ap, start=True, stop=True),
                                     [("mkT", b)] + qkeys, [sk])
                            else:
                                S.op("pe", lambda e, j=j, pS=pS, q_ap=q_ap: e.matmul(
                                    pS, lhsT=khT[:, j * 128:(j + 1) * 128], rhs=q_ap, start=True, stop=False),
                                     [("khT", j)] + qkeys, [sk])
                                if mode == "dsa":
                                    S.op("pe", lambda e, j=j, pS=pS, ht=has_tab: e.matmul(
                                        pS, lhsT=ident, rhs=mbT[:, j:j + 1, :].broadcast_to([128, 2, 256]), start=False, stop=(not ht)),
                                         ["ident", ("mbT", j // 4, 0), ("mbT", j // 4, 1)], [sk])
                                else:
                                    S.op("pe", lambda e, b=b, j=j, pS=pS, ht=has_tab: e.matmul(
                                        pS, lhsT=Esb[:, j // 2, :], rhs=mbmT[:, 2 * b:2 * b + 2, :], start=False, stop=(not ht)),
                                         ["E", ("mbmT", 0), ("mbmT", 1)], [sk])
                                if has_tab:
                                    S.op("pe", lambda e, b=b, j=j, pS=pS, tab0=tab0: e.matmul(
                                        pS, lhsT=ident, rhs=tabs[:, tab0 + 2 * b:tab0 + 2 * b + 2, j - (nj - 5), :], start=False, stop=True),
                                         ["ident", ("tabs", tab0 + 2 * b), ("tabs", tab0 + 2 * b + 1)], [sk])
                            pe_ = cnts["p16"] % 3
                            cnts["p16"] += 1
                            pk_ = "p16_%d" % pe_
                            S.op("act", lambda e, sb_=sb_, pe_=pe_: e.activation(out=p16[pe_], in_=ps[sb_][:, :], func=AF.Exp), [sk], [pk_])
                            flush_deferred(keep=1)

                            def pv(j=j, pe_=pe_, pk_=pk_, b=b, njj=njj, mode=mode):
                                for wh in range(2):
                                    for slot in range(2):
                                        if mode == "mem":
                                            rhs = memV[:, j, 2 * b + wh, :]
                                            rk = [("memV", 0), ("memV", 1)]
                                        else:
                                            rhs = vh[wh][:, j * 65:(j + 1) * 65]
                                            rk = [("vh", wh, j)]
                                        q4 = wh * 2 + slot
                                        S.op("pe", lambda e, q4=q4, rhs=rhs: e.matmul(
                                            ps[pob][:, q4 * 65:(q4 + 1) * 65], lhsT=p16[pe_][:, q4 * 128:(q4 + 1) * 128],
                                            rhs=rhs, start=(j == 0 and q4 == 0), stop=(j == njj - 1), skip_group_check=True),
                                             [pk_] + rk, [pok])
                            entry = [pv]
                            if j == njj - 1:
                                def norm(b=b, yb_=yb_, ycol0=ycol0, nh=nh, npair=npair, mode=mode):
                                    prb = 0
                                    prk = "gp"
                                    S.op("act", lambda e: e.activation(out=praw[prb], in_=ps[pob][:, 0:260], func=AF.Copy), [pok], [prk])
                                    S.op("dve", lambda e: e.reciprocal(
                                        out=rden, in_=praw[prb].rearrange("p (q d) -> p q d", q=4)[:, :, 64]), [prk], ["rden"])
                                    for wh in range(2):
                                        for slot in range(2):
                                            q4 = wh * 2 + slot
                                            hd = 2 * b + wh
                                            S.op("dve", lambda e, slot=slot, q4=q4, hd=hd: e.tensor_scalar(
                                                out=ystg[yb_][:, slot, 64 * hd:64 * hd + 64], in0=praw[prb][:, q4 * 65:q4 * 65 + 64],
                                                scalar1=rden[:, q4:q4 + 1], scalar2=None, op0=ALU.mult),
                                                 [prk, "rden"], [("ystg", yb_, slot, hd)])
                                    if b == npair - 1:
                                        ykeys = [("ystg", yb_, s_, h_) for s_ in range(2) for h_ in range(nh)]
                                        S.dma("pool", "ystore%d" % yb_, lambda e: e.dma_start(
                                            out=YS[t0:t0 + 256, ycol0:ycol0 + 64 * nh].rearrange("(s p) d -> p s d", p=128),
                                            in_=ystg[yb_][:, :, 0:64 * nh]), reads=ykeys, writes=ykeys + [("YS", m, mode)])
                                entry.append(norm)
                            deferred.append(entry)
                            yield 0.65
                    if mode == "dsa":
                        yield ("dsa_done",)

            def gen_indexer(m):
                L = 512 * (m + 1)
                nsg = m + 1
                t0 = 256 * m
                S.dma("sp", "iq", lambda e: e.dma_start(
                    out=iq_sb[0:64], in_=QT[768:1280, t0:t0 + 256].rearrange("(h d) t -> d h t", d=64)), writes=["iq"])
                sckeys = [("sc", sg) for sg in range(nsg)]
                for slot in range(2):
                    sl = 2 * m + slot
                    for hh in range(8):
                        S.op("dve", lambda e, hh=hh, sl=sl: e.tensor_scalar(
                            out=dgw[:, hh, :], in0=ident, scalar1=iw_all[:, sl * 8 + hh:sl * 8 + hh + 1], scalar2=None,
                            op0=ALU.mult), ["ident", ("iw", sl)], [("dg", hh)])
                    for sg in range(nsg):
                        blk = sc[:, sg * 512:(sg + 1) * 512]
                        ab = (4, 6)[cnts["acc"] % 2]
                        cnts["acc"] += 1
                        ak = "ps%d" % ab
                        pq = []
                        for hh in range(8):
                            pb = cnts["ips"] % 4
                            cnts["ips"] += 1
                            rb = 3 + cnts["r16"] % 4
                            cnts["r16"] += 1
                            rk_ = "p16_%d" % rb
                            S.op("pe", lambda e, hh=hh, sg=sg, pb=pb, slot=slot: e.matmul(
                                ps[pb][:, :], lhsT=iq_sb[:, hh, slot * 128:(slot + 1) * 128], rhs=ikT[:, sg * 512:(sg + 1) * 512],
                                start=True, stop=True), ["iq", "ikT"], ["ps%d" % pb])
                            if hh % 2 == 0:
                                S.op("act", lambda e, pb=pb, rb=rb: e.activation(out=p16[rb], in_=ps[pb][:, :], func=AF.Relu),
                                     ["ps%d" % pb], [rk_])
                            else:
                                S.op("dve", lambda e, pb=pb, rb=rb: e.tensor_scalar(out=p16[rb], in0=ps[pb][:, :], scalar1=0.0,
                                                                                   scalar2=None, op0=ALU.max),
                                     ["ps%d" % pb], [rk_])

                            def dmm(hh=hh, rb=rb, rk_=rk_, ab=ab, ak=ak):
                                S.op("pe", lambda e: e.matmul(ps[ab][:, :], lhsT=dgw[:, hh, :], rhs=p16[rb], start=(hh == 0), stop=(hh == 7)),
                                     [("dg", hh), rk_], [ak])
                            pq.append(dmm)
                            while len(pq) > 3:
                                pq.pop(0)()
                            yield ("s", 0.45)
                        while pq:
                            pq.pop(0)()
                        S.op("act", lambda e, blk=blk, ab=ab: e.activation(out=blk, in_=ps[ab][:, :], func=AF.Copy), [ak], [("sc", sg)])
                        S.op("dve", lambda e, blk=blk, sg=sg: e.tensor_reduce(out=bmax[:, sg:sg + 1], in_=blk, axis=AX.X, op=ALU.max),
                             [("sc", sg)], [("bmax", sg)])
                        S.op("dve", lambda e, blk=blk, sg=sg: e.tensor_reduce(out=bmin[:, sg:sg + 1], in_=blk, axis=AX.X, op=ALU.min),
                             [("sc", sg)], [("bmin", sg)])
                    S.op("dve", lambda e: e.tensor_reduce(out=rmax, in_=bmax[:, 0:nsg], axis=AX.X, op=ALU.max),
                         [("bmax", g_) for g_ in range(nsg)], ["rmax"])
                    S.op("dve", lambda e: e.tensor_reduce(out=lo, in_=bmin[:, 0:nsg], axis=AX.X, op=ALU.min),
                         [("bmin", g_) for g_ in range(nsg)], ["lo"])
                    S.op("dve", lambda e, slot=slot: e.tensor_tensor(out=sc[:, L - 512:L], in0=sc[:, L - 512:L],
                                                                   in1=cmask_sb[:, slot, :], op=ALU.add),
                         ["cmask"], [("sc", nsg - 1)])
                    S.op("dve", lambda e: e.tensor_tensor(out=w0, in0=rmax, in1=lo, op=ALU.subtract), ["rmax", "lo"], ["w0"])
                    yield ("b", 1.0)
                    for k in range(NBIS):
                        c = 2.0 ** -(k + 1)
                        S.op("dve", lambda e, c=c: e.scalar_tensor_tensor(out=mid, in0=w0, scalar=c, in1=lo, op0=ALU.mult,
                                                                        op1=ALU.add), ["w0", "lo"], ["mid"])
                        S.op("dve", lambda e: e.tensor_scalar(out=mb[:, 0:L], in0=sc[:, 0:L], scalar1=mid, scalar2=0.0,
                                                             op0=ALU.is_ge, op1=ALU.add, accum_out=cnt),
                             sckeys + ["mid"], ["mb", "cnt"])
                        S.op("dve", lambda e, c=c: e.tensor_scalar(out=step, in0=cnt, scalar1=255.5, scalar2=c, op0=ALU.is_ge,
                                                                  op1=ALU.mult), ["cnt"], ["step"])
                        S.op("dve", lambda e: e.scalar_tensor_tensor(out=lo, in0=step, scalar=w0, in1=lo, op0=ALU.mult,
                                                                    op1=ALU.add), ["step", "w0", "lo"], ["lo"])
                        yield ("b", L * 1.1e-3 + 0.6)
                    S.op("dve", lambda e: e.tensor_scalar(out=mb[:, 0:L], in0=sc[:, 0:L], scalar1=lo, scalar2=NEG,
                                                         op0=ALU.is_lt, op1=ALU.mult), sckeys + ["lo"], ["mb"])
                    yield ("b", L * 0.6e-3)
                    yield ("maskT", slot)

            def emit_maskT(m, slot):
                nj = 4 * (m + 1)
                for jg in range(nj // 4):
                    tb_ = (7, 6)[jg % 2]
                    tk_ = "ps%d" % tb_
                    pT = ps[tb_][:].bitcast(BF16)[:, 0:512].rearrange("p (a t) -> p a t", a=4)
                    for jj in range(4):
                        j = 4 * jg + jj
                        S.op("pe", lambda e, jj=jj, j=j, pT=pT: e.transpose(out=pT[:, jj, :], in_=mb[:, j * 128:(j + 1) * 128],
                                                                          identity=ident), ["mb", "ident"], [tk_])
                    evac(mbT[:, 4 * jg:4 * jg + 4, slot * 128:(slot + 1) * 128], pT, [tk_], [("mbT", jg, slot)])

            def drive(ga, gb, mb_chunk, a_total=0.0):
                Lb = 512 * (mb_chunk + 1)
                bis_total = 2 * (NBIS * (Lb * 1.1e-3 + 0.6) + 1.0 + Lb * 0.6e-3)
                score_total = 2 * (mb_chunk + 1) * 8 * 0.45
                ratio = max(0.0, (a_total - bis_total) / score_total) if gb is not None else 0.0
                ta = tb = 0.0
                a_done = ga is None
                dsa_done = ga is None
                b_done = gb is None
                b_wait = None
                while not (a_done and b_done):
                    if b_wait is not None and dsa_done:
                        emit_maskT(mb_chunk, b_wait)
                        b_wait = None
                        continue
                    pick_a = (not a_done) and (b_done or b_wait is not None or ta < tb)
                    if pick_a:
                        try:
                            c = next(ga)
                            if isinstance(c, tuple):
                                dsa_done = True
                            else:
                                ta += c
                        except StopIteration:
                            a_done = True
                            dsa_done = True
                            flush_deferred()
                    else:
                        try:
                            c = next(gb)
                            if c[0] == "maskT":
                                b_wait = c[1]
                            elif c[0] == "s":
                                tb += c[1] * ratio
                            else:
                                tb += c[1]
                        except StopIteration:
                            b_done = True

            def chain(*gens):
                for g in gens:
                    if g is not None:
                        yield from g

            def gate_now(m):
                gate_part1(m)
                gate_part2()
                yield 5.0

            drive(chain(gate_now(0), gen_attention(0, ("moba", "mem"))), gen_indexer(0), 0, a_total=5.0 + 16 * 0.65)
            for m in range(nchunks):
                nxt = m + 1 if m + 1 < nchunks else None
                ga = chain(gen_attention(m, ("dsa",), gate_for=nxt),
                           gen_attention(nxt, ("moba", "mem")) if nxt is not None else None)
                a_total = 0.65 * (3 * 4 * (m + 1) + (3 * 4 * (m + 2) + 4 if nxt is not None else 0))
                drive(ga, gen_indexer(nxt) if nxt is not None else None, m + 1, a_total=a_total)
            S.barrier()
            A.top = m2

        if "p3" in phases:
            m3 = A.top
            wzg = A.bf16(8 * 4096).rearrange("p (k n) -> p k n", k=8)
            wbr = A.bf16(8 * D).rearrange("p (k n) -> p k n", k=8)
            wo = A.bf16(8 * D).rearrange("p (k n) -> p k n", k=8)
            fg = A.f32(D)
            mw = A.top
            wstg = [A.f32(4096) for _ in range(2)]
            load_weight(w_zg, 4096, wzg, gain_sb, "wzg", wstg)
            load_weight(w_br, D, wbr, None, "wbr", wstg)
            load_weight(w_o, D, wo, None, "wo", wstg)
            S.dma("sp", "s_fg", lambda e: e.dma_start(out=fg, in_=fgain), writes=["fg"])
            S.barrier()
            A.top = mw
            xt3 = [A.f32(D).rearrange("p (a d) -> p a d", a=1) for _ in range(2)]
            yt3 = [A.f32(D) for _ in range(2)]
            xs3 = A.bf16(D).rearrange("p (a d) -> p a d", a=1)
            junk3 = A.bf16(D)
            junk4 = A.bf16(D)
            hT3 = A.bf16(8 * 128).rearrange("p (k t) -> p k t", k=8)
            zg2 = [A.f32(4096) for _ in range(2)]
            u3 = A.bf16(D)
            uT = A.bf16(8 * 128).rearrange("p (k t) -> p k t", k=8)
            merged = A.f32(D)
            tmp3 = A.f32(512)
            mgb = A.bf16(D)
            mT = A.bf16(8 * 128).rearrange("p (k t) -> p k t", k=8)
            r3 = A.f32(D).rearrange("p (a d) -> p a d", a=1)
            ostg = [A.f32(D) for _ in range(2)]
            ss3 = A.f32(1)
            ms3 = A.f32(1)
            rs3 = A.f32(1)
            ss4 = A.f32(1)
            ms4 = A.f32(1)
            rs4 = A.f32(1)
            ntile3 = 2 * nchunks

            def transpose8(src, dst, skey, dkey, banks):
                for half in range(2):
                    pb = banks[half]
                    pT = ps[pb][:].bitcast(BF16)[:, 0:512].rearrange("p (a t) -> p a t", a=4)
                    for a in range(4):
                        kc = half * 4 + a
                        S.op("pe", lambda e, a=a, kc=kc, pT=pT: e.transpose(out=pT[:, a, :], in_=src[:, kc * 128:(kc + 1) * 128],
                                                                          identity=ident), skey + ["ident"], ["ps%d" % pb])
                    evac(dst[:, half * 4:half * 4 + 4, :], pT, ["ps%d" % pb], [(dkey, half)])

            def p3_front(i, part):
                b = i % 2
                xk = "x3_%d" % b
                yk = "y3_%d" % b
                zgb = zg2[b]
                if part == 0:
                    S.dma("sp", xk, lambda e: e.dma_start(out=xt3[b][:, 0, :], in_=xown[i * 128:(i + 1) * 128, :]), writes=[xk])
                    S.dma("sp", yk, lambda e: e.dma_start(out=yt3[b], in_=YS[i * 128:(i + 1) * 128, :]), writes=[yk])
                    rms_tile(xt3[b], 1, xs3, junk3, ss3, ms3, rs3, [xk], "p3")
                    transpose8(xs3[:, 0, :], hT3, [("p3xs", 0)], "hT3", (6, 7))
                hkeys = [("hT3", 0), ("hT3", 1)]
                for grp in range(4 * part, 4 * part + 4):
                    pb = grp % 3
                    for kc in range(8):
                        S.op("pe", lambda e, kc=kc, pb=pb, grp=grp: e.matmul(
                            ps[pb][:, :], lhsT=hT3[:, kc, :], rhs=wzg[:, kc, grp * 512:(grp + 1) * 512],
                            start=(kc == 0), stop=(kc == 7)), [("wzg", kc)] + hkeys, ["ps%d" % pb])
                    fn = AF.Silu if grp < 2 else AF.Sigmoid
                    S.op("act", lambda e, pb=pb, grp=grp, fn=fn: e.activation(out=zgb[:, grp * 512:(grp + 1) * 512], in_=ps[pb][:, :],
                                                                           func=fn), ["ps%d" % pb], [("zg", b, grp)])

            def p3_back(i, part):
                b = i % 2
                xk = "x3_%d" % b
                yk = "y3_%d" % b
                zgb = zg2[b]
                if part == 0:
                    p3_back_a(i, b, xk, yk, zgb)
                else:
                    p3_back_b(i, b, xk, yk, zgb)

            def p3_back_a(i, b, xk, yk, zgb):
                for half in range(2):
                    S.op("dve", lambda e, half=half: e.tensor_tensor(
                        out=u3[:, half * 512:(half + 1) * 512], in0=yt3[b][:, half * 512:(half + 1) * 512],
                        in1=zgb[:, half * 512:(half + 1) * 512], op=ALU.mult), [yk, ("zg", b, half)], [("u3", half)])
                transpose8(u3, uT, [("u3", 0), ("u3", 1)], "uT", (5, 4))
                ukeys = [("uT", 0), ("uT", 1)]
                blks = [(0, 3), (3, 6), (6, 8)]
                for dgi in range(2):
                    for bi, (b0, b1) in enumerate(blks):
                        pb = 3 + (dgi * 3 + bi) % 3
                        for kb in range(b0, b1):
                            S.op("pe", lambda e, kb=kb, pb=pb, dgi=dgi, b0=b0, b1=b1: e.matmul(
                                ps[pb][:, :], lhsT=uT[:, kb, :], rhs=wbr[:, kb, dgi * 512:(dgi + 1) * 512],
                                start=(kb == b0), stop=(kb == b1 - 1)), [("wbr", kb)] + ukeys, ["ps%d" % pb])
                        sg_ap = zgb[:, 1024 + bi * 1024 + dgi * 512:1024 + bi * 1024 + (dgi + 1) * 512]
                        sgk = ("zg", b, 2 + bi * 2 + dgi)
                        mg_ap = merged[:, dgi * 512:(dgi + 1) * 512]
                        if bi == 0:
                            S.op("dve", lambda e, pb=pb, sg_ap=sg_ap, mg_ap=mg_ap: e.tensor_tensor(
                                out=mg_ap, in0=ps[pb][:, :], in1=sg_ap, op=ALU.mult), ["ps%d" % pb, sgk], [("mg", dgi)])
                        else:
                            S.op("dve", lambda e, pb=pb, sg_ap=sg_ap: e.tensor_tensor(
                                out=tmp3, in0=ps[pb][:, :], in1=sg_ap, op=ALU.mult), ["ps%d" % pb, sgk], ["tmp3"])
                            o_ap = mg_ap if bi == 1 else mgb[:, dgi * 512:(dgi + 1) * 512]
                            okey = ("mg", dgi) if bi == 1 else ("mgb", dgi)
                            S.op("dve", lambda e, mg_ap=mg_ap, o_ap=o_ap: e.tensor_tensor(
                                out=o_ap, in0=mg_ap, in1=tmp3, op=ALU.add), ["tmp3", ("mg", dgi)], [okey])

            def p3_back_b(i, b, xk, yk, zgb):
                transpose8(mgb, mT, [("mgb", 0), ("mgb", 1)], "mT", (5, 4))
                mkeys = [("mT", 0), ("mT", 1)]
                for eg in range(2):
                    pb = 3 + eg
                    for kc in range(8):
                        S.op("pe", lambda e, kc=kc, pb=pb, eg=eg: e.matmul(
                            ps[pb][:, :], lhsT=mT[:, kc, :], rhs=wo[:, kc, eg * 512:(eg + 1) * 512],
                            start=(kc == 0), stop=(kc == 7)), [("wo", kc)] + mkeys, ["ps%d" % pb])
                    S.op("dve", lambda e, pb=pb, eg=eg: e.tensor_tensor(
                        out=r3[:, 0, eg * 512:(eg + 1) * 512], in0=ps[pb][:, :], in1=xt3[b][:, 0, eg * 512:(eg + 1) * 512],
                        op=ALU.add), ["ps%d" % pb, xk], [("r3", eg)])
                rms_tile(r3, 1, None, junk4, ss4, ms4, rs4, [("r3", 0), ("r3", 1)], "p3f")
                ok = "ostg%d" % b
                S.op("dve", lambda e: e.scalar_tensor_tensor(out=ostg[b], in0=r3[:, 0, :], scalar=rs4[:, 0:1], in1=fg,
                                                            op0=ALU.mult, op1=ALU.mult),
                     [("r3", 0), ("r3", 1), "p3frstd", "fg"], [ok])
                S.dma("pool", ok, lambda e: e.dma_start(out=out[i * 128:(i + 1) * 128, :], in_=ostg[b]),
                      reads=[ok], writes=[("out", i)])

            p3_front(0, 0)
            p3_front(0, 1)
            for i in range(ntile3):
                if i + 1 < ntile3:
                    p3_front(i + 1, 0)
                p3_back(i, 0)
                if i + 1 < ntile3:
                    p3_front(i + 1, 1)
                p3_back(i, 1)
            A.top = m3
        S.barrier()
        with nc.Block() as block:
            S.emit(block)
    return nc


def np_bucket(dist):
    n = np.maximum(dist, 0)
    nf = np.maximum(n, 1).astype(np.float32)
    large = 16 + (np.log(nf / np.float32(16)) / np.float32(math.log(128 / 16)) * np.float32(16)).astype(np.int32)
    return np.where(n < 16, n, np.minimum(large, 31))


def own_rows(h):
    return np.concatenate([np.arange(256 * (2 * m + h), 256 * (2 * m + h) + 256) for m in range(NCH)])


def make_in_maps(x, mem, norm_gain, w_in, rel_bias, mem_norm_gain, w_mem_kv, w_branch, w_out, final_norm_gain):
    f = np.float32
    w = np.asarray(w_in[0], f)

    def cols(*names):
        return np.ascontiguousarray(np.concatenate([w[:, OFF[n][0]:OFF[n][1]] for n in names], axis=1))

    shared = {
        "gain": np.ascontiguousarray(np.asarray(norm_gain[0], f).reshape(8, 128).T),
        "mgain": np.ascontiguousarray(np.asarray(mem_norm_gain[0], f).reshape(8, 128).T),
        "fgain": np.ascontiguousarray(np.broadcast_to(np.asarray(final_norm_gain, f)[None, :], (128, D))),
        "w_kvT": cols("ka", "kb", "ik"),
        "w_v": cols("va", "vb"),
        "w_qT": cols("qa", "qb", "iq", "qm"),
        "w_iw": cols("iw"),
        "w_zg": cols("za", "zb", "zm", "ga", "gb", "gm"),
        "w_mkv": np.ascontiguousarray(np.asarray(w_mem_kv[0], f)),
        "w_br": np.ascontiguousarray(np.asarray(w_branch[0], f)),
        "w_o": np.ascontiguousarray(np.asarray(w_out[0], f)),
        "b31": np.ascontiguousarray(np.broadcast_to(np.asarray(rel_bias, f)[31][None, :], (128, 12))),
        "ident": np.eye(128, dtype=f).astype(ml_dtypes.bfloat16),
    }
    E = np.zeros((32, 32, 128), f)
    for n in range(32):
        E[n, n, :] = 1.0
    shared["E"] = E.reshape(32, 32 * 128).astype(ml_dtypes.bfloat16)
    rb = np.asarray(rel_bias, f)
    sl = np.arange(128)[:, None]
    tl = np.arange(256)[None, :]
    maps = []
    for c in range(8):
        b, h = c // 2, c % 2
        rows = own_rows(h)
        m = dict(shared)
        m["xall"] = np.ascontiguousarray(np.asarray(x[b], f))
        m["xown"] = np.ascontiguousarray(np.asarray(x[b], f)[rows])
        m["memx"] = np.ascontiguousarray(np.asarray(mem[b], f))
        tabraw = np.zeros((12, 128, 5, 256), f)
        cmt = np.zeros((128, 5, 256), f)
        for ii, i in enumerate(range(1, 6)):
            Di = 256 * h + 256 - 128 * i
            d = Di + tl - sl
            bk = np_bucket(d)
            tabraw[:, :, ii, :] = rb[bk].transpose(2, 0, 1)
            cmt[:, ii, :] = np.where(d < 0, NEG, 0.0)
        m["tabraw"] = tabraw
        m["cmt"] = cmt
        cm = np.zeros((128, 2, 512), f)
        k = np.arange(512)[None, :]
        for slot in range(2):
            lim = 256 * h + 128 * slot + np.arange(128)[:, None]
            cm[:, slot, :] = np.where(k <= lim, 0.0, NEG)
        m["cmask"] = cm.astype(ml_dtypes.bfloat16)
        pn = np.zeros((128, 16, 32), f)
        no = np.full((128, 16, 32), NEG, f)
        for mm in range(16):
            own = 2 * mm + h
            pn[:, mm, own:] = NEG
            no[:, mm, own] = 0.0
        m["pastneg"] = pn.astype(ml_dtypes.bfloat16)
        m["notown"] = no.astype(ml_dtypes.bfloat16)
        maps.append(m)
    return maps


_NC_CACHE = {}


def kernel(x, mem, norm_gain, w_in, rel_bias, mem_norm_gain, w_mem_kv, w_branch, w_out, final_norm_gain):
    maps = make_in_maps(x, mem, norm_gain, w_in, rel_bias, mem_norm_gain, w_mem_kv, w_branch, w_out, final_norm_gain)
    if "nc" not in _NC_CACHE:
        _NC_CACHE["nc"] = build_program()
    res = run_bass_kernel_spmd(_NC_CACHE["nc"], maps, core_ids=list(range(8)))
    outf = np.zeros((4, T, D), np.float32)
    for c in range(8):
        b, h = c // 2, c % 2
        outf[b, own_rows(h)] = res.results[c]["out"]
    return outf
```

```python
import math
from contextlib import ExitStack

import numpy as np
import ml_dtypes

import concourse.bass as bass
import concourse.mybir as mybir
from concourse.bass_utils import run_bass_kernel_spmd

F32 = mybir.dt.float32
BF16 = mybir.dt.bfloat16
AF = mybir.ActivationFunctionType
ALU = mybir.AluOpType
AX = mybir.AxisListType

T = 8192
D = 1024
NOWN = 4096
NCH = 16
NEG = -30000.0
EPS = 1e-6
NBIS = 14
ARENA = 53200

OFF = {}
_o = 0
for _n, _s in [("qa", 384), ("ka", 384), ("va", 384), ("za", 384), ("qb", 384), ("kb", 384), ("vb", 384),
               ("zb", 384), ("iq", 512), ("ik", 64), ("iw", 8), ("qm", 256), ("zm", 256), ("ga", 1024),
               ("gb", 1024), ("gm", 1024)]:
    OFF[_n] = (_o, _o + _s)
    _o += _s
assert _o == 7240


class Sched:
    ENG = ("pe", "act", "dve", "pool", "sp")
    HND = {"pe": "tensor", "act": "scalar", "dve": "vector", "pool": "gpsimd", "sp": "sync"}

    def __init__(self, nc, stack):
        self.nc = nc
        self.stack = stack
        self.q = {e: [] for e in self.ENG}
        self.cnt = {e: 0 for e in self.ENG}
        self.semobj = {}
        for e in self.ENG:
            self.semobj["prog_" + e] = stack.enter_context(nc.semaphore("prog_" + e))
        self.dcount = {}
        self.lastw = {}
        self.readers = {}
        self.waited = {e: {} for e in self.ENG}

    def _deps(self, reads, writes):
        deps = []
        for k in reads:
            d = self.lastw.get(k)
            if d is not None:
                deps.append(d)
        for k in writes:
            d = self.lastw.get(k)
            if d is not None:
                deps.append(d)
            deps.extend(self.readers.get(k, ()))
        return deps

    def _commit(self, dep, reads, writes):
        for k in writes:
            self.lastw[k] = dep
            self.readers[k] = []
        for k in reads:
            self.readers.setdefault(k, []).append(dep)

    def _waits(self, eng, deps):
        need = {}
        for (sk, val, deng) in deps:
            if deng == eng and eng == "pe":
                continue
            if self.waited[eng].get(sk, 0) >= val:
                continue
            if need.get(sk, 0) < val:
                need[sk] = val
        for sk, val in need.items():
            self.waited[eng][sk] = val
        return list(need.items())

    def op(self, eng, fn, reads=(), writes=()):
        writes = list(writes) + [k for k in reads if isinstance(k, str) and k.startswith("ps") and k not in writes]
        waits = self._waits(eng, self._deps(reads, writes))
        self.cnt[eng] += 1
        sk = "prog_" + eng
        self.q[eng].append((waits, fn, (sk, 1)))
        dep = (sk, self.cnt[eng], eng)
        self._commit(dep, reads, writes)
        return dep

    def dma(self, eng, slot, fn, reads=(), writes=()):
        sk = "d_" + slot
        if sk not in self.semobj:
            self.semobj[sk] = self.stack.enter_context(self.nc.semaphore(sk))
            self.dcount[sk] = 0
        waits = self._waits(eng, self._deps(reads, writes))
        self.dcount[sk] += 16
        self.q[eng].append((waits, fn, (sk, 16)))
        dep = (sk, self.dcount[sk], "dma")
        self._commit(dep, reads, writes)
        return dep

    def barrier(self):
        deps = [("prog_" + e, self.cnt[e], e + "_b") for e in self.ENG if self.cnt[e] > 0]
        deps += [(sk, v, "dma") for sk, v in self.dcount.items() if v > 0]
        for e in self.ENG:
            waits = self._waits(e, [d for d in deps if d[0] != "prog_" + e])
            if waits:
                self.q[e].append((waits, None, None))

    def final_wait(self, eng, keys):
        deps = [self.lastw[k] for k in keys if k in self.lastw]
        waits = self._waits(eng, deps)
        self.q[eng].append((waits, None, None))

    def emit(self, block):
        def mk(ename):
            items = self.q[ename]

            def body(e):
                for waits, fn, inc in items:
                    for sk, val in waits:
                        e.wait_ge(self.semobj[sk], val)
                    if fn is not None:
                        fn(e).then_inc(self.semobj[inc[0]], inc[1])
            return body
        for ename in self.ENG:
            getattr(block, self.HND[ename])(mk(ename))


class Arena:
    def __init__(self, t, ncols):
        self.t = t
        self.n = ncols
        self.top = 0

    def f32(self, cols, parts=128):
        cols_al = (cols + 15) // 16 * 16
        off = self.top
        self.top += cols_al
        assert self.top <= self.n, ("arena overflow", self.top)
        return self.t[0:parts, off:off + cols]

    def bf16(self, cols, parts=128):
        c32 = ((cols + 1) // 2 + 15) // 16 * 16
        off = self.top
        self.top += c32
        assert self.top <= self.n, ("arena overflow", self.top)
        return self.t[0:parts, off:off + c32].bitcast(BF16)[:, 0:cols]


def build_program(dbg=False, nchunks=NCH, phases=("p1", "p2", "p3"), nt1a=16, nt1b=8):
    nc = bass.Bass("TRN2", target_bir_lowering=False)

    def din(name, shape, dt=F32):
        return nc.dram_tensor(name, list(shape), dt, kind="ExternalInput").ap()

    skind = "ExternalOutput" if dbg else "Internal"

    def dscr(name, shape, dt):
        return nc.dram_tensor(name, list(shape), dt, kind=skind).ap()

    xall = din("xall", [T, D])
    xown = din("xown", [NOWN, D])
    memx = din("memx", [256, D])
    gain = din("gain", [128, 8])
    mgain = din("mgain", [128, 8])
    fgain = din("fgain", [128, D])
    w_kvT = din("w_kvT", [D, 832])
    w_v = din("w_v", [D, 768])
    w_qT = din("w_qT", [D, 1536])
    w_iw = din("w_iw", [D, 8])
    w_zg = din("w_zg", [D, 4096])
    w_mkv = din("w_mkv", [D, 512])
    w_br = din("w_br", [D, D])
    w_o = din("w_o", [D, D])
    tabraw = din("tabraw", [12, 128, 5, 256])
    cmt = din("cmt", [128, 5, 256])
    b31 = din("b31", [128, 12])
    cmask = din("cmask", [128, 2, 512], BF16)
    pastneg = din("pastneg", [128, 16, 32], BF16)
    notown = din("notown", [128, 16, 32], BF16)
    identd = din("ident", [128, 128], BF16)
    Ed = din("E", [32, 32 * 128], BF16)

    KT = dscr("KT", [832, T], BF16)
    VS = dscr("VS", [12, 128, 64 * 65], BF16)
    QT = dscr("QT", [1536, NOWN], BF16)
    YS = dscr("YS", [NOWN, D], F32)
    out = nc.dram_tensor("out", [NOWN, D], F32, kind="ExternalOutput").ap()

    with ExitStack() as st:
        S = Sched(nc, st)
        arena_t = st.enter_context(nc.sbuf_tensor("arena", [128, ARENA], F32))
        ps = [st.enter_context(nc.psum_tensor("ps%d" % i, [128, 512], F32)) for i in range(8)]
        A = Arena(arena_t, ARENA)

        ident = A.bf16(128)
        gain_sb = A.f32(8)
        mgain_sb = A.f32(8)
        iw_all = A.f32(32 * 8)
        kms = A.f32(3 * 32)
        S.dma("sp", "c0", lambda e: e.dma_start(out=ident, in_=identd), writes=["ident"])
        S.dma("sp", "c1", lambda e: e.dma_start(out=gain_sb, in_=gain), writes=["gain"])
        S.dma("sp", "c2", lambda e: e.dma_start(out=mgain_sb, in_=mgain), writes=["mgain"])
        S.op("pool", lambda e: e.memset(kms, 0.0), [], ["kms0"])
        base_top = A.top

        evac_rr = [0]

        def evac(out_ap, in_ap, reads, writes, scale=None, eng=None):
            if eng is None:
                eng = ("act", "dve")[evac_rr[0] % 2]
                evac_rr[0] += 1
            if eng == "act":
                if scale is None:
                    S.op("act", lambda e: e.activation(out=out_ap, in_=in_ap, func=AF.Copy), reads, writes)
                else:
                    S.op("act", lambda e: e.activation(out=out_ap, in_=in_ap, func=AF.Copy, scale=float(scale)),
                         reads, writes)
            else:
                if scale is None:
                    S.op("dve", lambda e: e.tensor_copy(out=out_ap, in_=in_ap), reads, writes)
                else:
                    S.op("dve", lambda e: e.tensor_scalar(out=out_ap, in0=in_ap, scalar1=float(scale), scalar2=None,
                                                          op0=ALU.mult), reads, writes)

        def load_weight(wd, ncols, dst, gain_ap, tag, stg):
            for kc in range(8):
                sb = stg[kc % 2]
                sk = "wstg%d" % (kc % 2)
                S.dma("sp", sk, lambda e, kc=kc, sb=sb: e.dma_start(out=sb[:, 0:ncols], in_=wd[kc * 128:(kc + 1) * 128, :]),
                      writes=[sk])
                if gain_ap is None:
                    evac(dst[:, kc, :], sb[:, 0:ncols], [sk], [(tag, kc)])
                else:
                    g = gain_ap[:, kc:kc + 1]
                    if kc % 2 == 0:
                        S.op("act", lambda e, kc=kc, sb=sb, g=g: e.activation(out=dst[:, kc, :], in_=sb[:, 0:ncols],
                                                                           func=AF.Copy, scale=g),
                             [sk, "gain", "mgain"], [(tag, kc)])
                    else:
                        S.op("dve", lambda e, kc=kc, sb=sb, g=g: e.tensor_scalar(out=dst[:, kc, :], in0=sb[:, 0:ncols],
                                                                              scalar1=g, scalar2=None, op0=ALU.mult),
                             [sk, "gain", "mgain"], [(tag, kc)])

        def rms_tile(x_ap, nsub, xs_out, junk, ss, ms, rstd, rkeys, tag):
            for a in range(nsub):
                S.op("act", lambda e, a=a: e.activation(out=junk, in_=x_ap[:, a, :], func=AF.Square,
                                                       accum_out=ss[:, a:a + 1]),
                     rkeys, [tag + "junk", (tag + "ss", a)])
            S.op("dve", lambda e: e.tensor_scalar(out=ms[:, 0:nsub], in0=ss[:, 0:nsub], scalar1=1.0 / D, scalar2=EPS,
                                                  op0=ALU.mult, op1=ALU.add),
                 [(tag + "ss", a) for a in range(nsub)], [tag + "ms"])
            S.op("act", lambda e: e.activation(out=ms[:, 0:nsub], in_=ms[:, 0:nsub], func=AF.Sqrt),
                 [tag + "ms"], [tag + "ms"])
            S.op("dve", lambda e: e.reciprocal(out=rstd[:, 0:nsub], in_=ms[:, 0:nsub]), [tag + "ms"], [tag + "rstd"])
            if xs_out is not None:
                for a in range(nsub):
                    S.op("dve", lambda e, a=a: e.tensor_scalar(out=xs_out[:, a, :], in0=x_ap[:, a, :],
                                                              scalar1=rstd[:, a:a + 1], scalar2=None, op0=ALU.mult),
                         rkeys + [tag + "rstd"], [(tag + "xs", a)])

        def proj_phase(xsrc, ntiles, wT, featT, wtok, tok_fn, tag):
            m0 = A.top
            xt = [A.f32(4 * D).rearrange("p (a d) -> p a d", a=4) for _ in range(2)]
            xs = A.bf16(4 * D).rearrange("p (a d) -> p a d", a=4)
            junk = A.bf16(D)
            hT = [A.bf16(8 * 512).rearrange("p (k t) -> p k t", k=8) for _ in range(2)]
            stg = [A.bf16(512) for _ in range(3)]
            ss = A.f32(4)
            ms = A.f32(4)
            rstd = A.f32(4)
            fb = 0
            for i in range(ntiles):
                b = i % 2
                xk = tag + "xt%d" % b
                S.dma("sp", xk, lambda e, i=i, b=b: e.dma_start(
                    out=xt[b], in_=xsrc[i * 512:(i + 1) * 512, :].rearrange("(a p) d -> p a d", p=128)), writes=[xk])
                rms_tile(xt[b], 4, xs, junk, ss, ms, rstd, [xk], tag)
                hk = tag + "hT%d" % b
                for kc in range(8):
                    pb = 6 + (kc % 2)
                    pT = ps[pb][:].bitcast(BF16)[:, 0:512].rearrange("p (a t) -> p a t", a=4)
                    for a in range(4):
                        S.op("pe", lambda e, a=a, kc=kc, pT=pT: e.transpose(out=pT[:, a, :],
                                                                          in_=xs[:, a, kc * 128:(kc + 1) * 128],
                                                                          identity=ident),
                             [(tag + "xs", a), "ident"], ["ps%d" % pb])
                    evac(hT[b][:, kc, :], ps[pb][:].bitcast(BF16)[:, 0:512], ["ps%d" % pb], [(hk, kc)])
                hkeys = [(hk, kc) for kc in range(8)]
                for (c0, ncol, r0, scale, kblk, dst) in featT:
                    pb = fb % 3
                    fb += 1
                    pk = "ps%d" % pb
                    for kc in range(8):
                        S.op("pe", lambda e, kc=kc, pb=pb, c0=c0, ncol=ncol, b=b: e.matmul(
                            ps[pb][0:ncol, :], lhsT=wT[:, kc, c0:c0 + ncol], rhs=hT[b][:, kc, :],
                            start=(kc == 0), stop=(kc == 7)),
                             [(tag + "wT", kc), (hk, kc)], [pk])
                    sb = stg[fb % 3]
                    sk = tag + "stg%d" % (fb % 3)
                    evac(sb[0:ncol, :], ps[pb][0:ncol, :], [pk], [sk], scale=scale)
                    if kblk is not None:
                        S.op("dve", lambda e, pb=pb, kblk=kblk, i=i: e.tensor_reduce(
                            out=kms[:, kblk * 32 + 2 * i: kblk * 32 + 2 * i + 2],
                            in_=ps[pb][:, :].rearrange("p (n s) -> p n s", s=256), axis=AX.X, op=ALU.add),
                             [pk, "kms0"], [("kms", kblk, i), pk])
                    S.dma("pool", sk, lambda e, sb=sb, r0=r0, ncol=ncol, i=i, dst=dst: e.dma_start(
                        out=dst[r0:r0 + ncol, i * 512:(i + 1) * 512], in_=sb[0:ncol, :]), reads=[sk], writes=[(tag + "featdst", r0, i)])
                for a in range(4):
                    tok_fn(i, a, b, hk, hT[b])
            A.top = m0

        if "p1" in phases:
            m1 = A.top
            wkvT = A.bf16(8 * 832).rearrange("p (k n) -> p k n", k=8)
            wv = A.bf16(8 * 768).rearrange("p (k n) -> p k n", k=8)
            wstg = [A.f32(1536) for _ in range(2)]
            load_weight(w_kvT, 832, wkvT, gain_sb, "p1awT", wstg)
            load_weight(w_v, 768, wv, gain_sb, "p1awv", wstg)
            vstg = [A.bf16(4 * 12 * 65).rearrange("p (h a d) -> p h a d", a=4, h=12) for _ in range(2)]
            for vb_ in range(2):
                S.op("pool", lambda e, vb_=vb_: e.memset(vstg[vb_], 1.0), [], ["vstg%d" % vb_])
            featT = [(blk * 128, 128, blk * 128, None, (blk if blk < 3 else None), KT) for blk in range(6)]
            featT.append((768, 64, 768, None, None, KT))

            def tok_v(i, a, b, hk, hTb):
                vb_ = i % 2
                vk = "vstg%d" % vb_
                for g, (c0, ncol) in enumerate([(0, 512), (512, 256)]):
                    pb = 3 + (a * 2 + g) % 3
                    pk = "ps%d" % pb
                    for kc in range(8):
                        S.op("pe", lambda e, kc=kc, pb=pb, c0=c0, ncol=ncol, a=a: e.matmul(
                            ps[pb][:, 0:ncol], lhsT=hTb[:, kc, a * 128:(a + 1) * 128], rhs=wv[:, kc, c0:c0 + ncol],
                            start=(kc == 0), stop=(kc == 7)),
                             [("p1awv", kc), (hk, kc)], [pk])
                    h0 = c0 // 64
                    nh = ncol // 64
                    evac(vstg[vb_][:, h0:h0 + nh, a, 0:64], ps[pb][:, 0:ncol].rearrange("p (h d) -> p h d", d=64),
                         [pk, vk], [(vk, a, g)])
                if a == 3:
                    S.dma("pool", vk, lambda e, i=i, vb_=vb_: e.dma_start(
                        out=VS[:, :, i * 4 * 65:(i + 1) * 4 * 65].rearrange("h p x -> p h x"),
                        in_=vstg[vb_].rearrange("p h a d -> p h (a d)")),
                        reads=[(vk, aa, g) for aa in range(4) for g in range(2)], writes=[("VS", i)])

            proj_phase(xall, nt1a, wkvT, featT, wv, tok_v, "p1a")
            A.top = m1
            S.barrier()

            wqT = A.bf16(8 * 1536).rearrange("p (k n) -> p k n", k=8)
            wiw = A.bf16(8 * 8).rearrange("p (k n) -> p k n", k=8)
            wstg = [A.f32(1536) for _ in range(2)]
            load_weight(w_qT, 1536, wqT, gain_sb, "p1bwT", wstg)
            load_weight(w_iw, 8, wiw, gain_sb, "p1bwiw", wstg)
            featT = []
            for blk in range(12):
                scale = None if 6 <= blk < 10 else 0.125
                featT.append((blk * 128, 128, blk * 128, scale, None, QT))

            def tok_iw(i, a, b, hk, hTb):
                pb = 3 + a % 3
                pk = "ps%d" % pb
                for kc in range(8):
                    S.op("pe", lambda e, kc=kc, pb=pb, a=a: e.matmul(
                        ps[pb][:, 0:8], lhsT=hTb[:, kc, a * 128:(a + 1) * 128], rhs=wiw[:, kc, :],
                        start=(kc == 0), stop=(kc == 7)),
                         [("p1bwiw", kc), (hk, kc)], [pk])
                sl = 4 * i + a
                evac(iw_all[:, sl * 8:(sl + 1) * 8], ps[pb][:, 0:8], [pk], [("iw", sl)])

            proj_phase(xown, nt1b, wqT, featT, wiw, tok_iw, "p1b")
            A.top = m1
            S.barrier()


        if "p2" in phases:
            m2 = A.top
            ikT = A.bf16(T)
            tabs = A.bf16(12 * 5 * 256).rearrange("p (h i t) -> p h i t", h=12, i=5)
            Esb = A.bf16(32 * 128).rearrange("p (n s) -> p n s", n=32)
            b31_sb = A.f32(12)
            cmask_sb = A.bf16(2 * 512).rearrange("p (s k) -> p s k", s=2)
            pastneg_sb = A.bf16(16 * 32).rearrange("p (m n) -> p m n", m=16)
            notown_sb = A.bf16(16 * 32).rearrange("p (m n) -> p m n", m=16)
            kmbf = A.bf16(3 * 32).rearrange("p (b n) -> p b n", b=3)
            mbT = A.bf16(64 * 256).rearrange("p (j t) -> p j t", j=64)
            mbmT = A.bf16(6 * 256).rearrange("p (h t) -> p h t", h=6)
            iq_sb = A.bf16(8 * 256).rearrange("p (h t) -> p h t", h=8)
            qz = [A.bf16(nb_ * 512).rearrange("p (b t) -> p b t", b=nb_) for nb_ in (3, 3, 2)]
            gsb = A.f32(2 * 192).rearrange("p (a n) -> p a n", a=2)
            praw = [gsb.rearrange("p a n -> p (a n)")[:, 0:260]]
            ystg = [A.f32(2 * 384).rearrange("p (s d) -> p s d", s=2) for _ in range(2)]
            mkT = A.bf16(2 * 256).rearrange("p (b t) -> p b t", b=2)
            memV = A.bf16(2 * 4 * 65).rearrange("p (j h d) -> p j h d", j=2, h=4)
            smalls = A.f32(48)
            rmax = smalls[:, 0:1]
            lo = smalls[:, 1:2]
            w0 = smalls[:, 2:3]
            mid = smalls[:, 3:4]
            cnt = smalls[:, 4:5]
            step = smalls[:, 5:6]
            gm = A.f32(6 * 32).rearrange("p (h n) -> p h n", h=6)
            mx8 = A.f32(6 * 8).rearrange("p (h n) -> p h n", h=6)
            mbm = A.bf16(2 * 6 * 32).rearrange("p (s h n) -> p s h n", s=2, h=6)
            rden = smalls[:, 8:12]
            bmax = smalls[:, 16:32]
            bmin = smalls[:, 32:48]
            p16 = [A.bf16(512) for _ in range(7)]
            m_sc = A.top
            sc = A.f32(T)
            mb = A.bf16(T)
            dgw = A.bf16(8 * 128).rearrange("p (h t) -> p h t", h=8)
            khT = A.bf16(T)
            vh = [A.bf16(64 * 65) for _ in range(2)]
            mB = A.top
            A.top = m_sc

            S.op("pool", lambda e: e.memset(ikT, 0.0), [], ["ikT"])
            S.op("pool", lambda e: e.memset(Esb, 0.0), [], ["E"])
            S.op("pool", lambda e: e.memset(mbmT, 0.0), [], [("mbmT", 0), ("mbmT", 1)])
            S.op("pool", lambda e: e.memset(iq_sb, 0.0), [], ["iq"])
            S.dma("sp", "s_ik", lambda e: e.dma_start(out=ikT[0:64, 0:512 * nt1a], in_=KT[768:832, 0:512 * nt1a]), writes=["ikT"])
            S.dma("sp", "s_E", lambda e: e.dma_start(out=Esb[0:32], in_=Ed.rearrange("p (n s) -> p n s", n=32)), writes=["E"])
            S.dma("sp", "s_b31", lambda e: e.dma_start(out=b31_sb, in_=b31), writes=["b31"])
            S.dma("sp", "s_cm", lambda e: e.dma_start(out=cmask_sb, in_=cmask), writes=["cmask"])
            S.dma("sp", "s_pn", lambda e: e.dma_start(out=pastneg_sb, in_=pastneg), writes=["pastneg"])
            S.dma("sp", "s_no", lambda e: e.dma_start(out=notown_sb, in_=notown), writes=["notown"])
            cmt_sb = A.f32(5 * 256).rearrange("p (i t) -> p i t", i=5)
            tstg = [A.f32(5 * 256).rearrange("p (i t) -> p i t", i=5) for _ in range(2)]
            S.dma("sp", "s_cmt", lambda e: e.dma_start(out=cmt_sb, in_=cmt), writes=["cmt"])
            for hd in range(12):
                tk = "tstg%d" % (hd % 2)
                S.dma("sp", tk, lambda e, hd=hd: e.dma_start(out=tstg[hd % 2], in_=tabraw[hd]), writes=[tk])
                S.op("dve", lambda e, hd=hd: e.scalar_tensor_tensor(out=tabs[:, hd], in0=tstg[hd % 2],
                                                                   scalar=b31_sb[:, hd:hd + 1], in1=cmt_sb,
                                                                   op0=ALU.subtract, op1=ALU.add),
                     [tk, "b31", "cmt"], [("tabs", hd)])
            S.op("dve", lambda e: e.tensor_copy(out=kmbf, in_=kms.rearrange("p (b n) -> p b n", b=3)), [], ["kmbf"])
            for qi in range(3):
                S.op("pool", lambda e, qi=qi: e.memset(qz[qi], 0.0), [], ["qzero%d" % qi])
            xm = A.f32(2 * D).rearrange("p (a d) -> p a d", a=2)
            xsm = A.bf16(2 * D).rearrange("p (a d) -> p a d", a=2)
            junkm = A.bf16(D)
            hTm = A.bf16(8 * 256).rearrange("p (k t) -> p k t", k=8)
            wmk = A.bf16(8 * 512).rearrange("p (k n) -> p k n", k=8)
            ssm = A.f32(2)
            msm = A.f32(2)
            rsm = A.f32(2)
            wstg = [A.f32(512) for _ in range(2)]
            load_weight(w_mkv, 512, wmk, mgain_sb, "wmk", wstg)
            S.dma("sp", "s_xm", lambda e: e.dma_start(out=xm, in_=memx.rearrange("(a p) d -> p a d", p=128)), writes=["xm"])
            rms_tile(xm, 2, xsm, junkm, ssm, msm, rsm, ["xm"], "mem")
            for kc in range(8):
                pb = 3 + kc % 2
                pT = ps[pb][:].bitcast(BF16)[:, 0:256].rearrange("p (a t) -> p a t", a=2)
                for a in range(2):
                    S.op("pe", lambda e, a=a, kc=kc, pT=pT: e.transpose(out=pT[:, a, :], in_=xsm[:, a, kc * 128:(kc + 1) * 128],
                                                                      identity=ident),
                         [("memxs", a), "ident"], ["ps%d" % pb])
                evac(hTm[:, kc, :], ps[pb][:].bitcast(BF16)[:, 0:256], ["ps%d" % pb], [("hTm", kc)])
            for blk in range(2):
                pb = blk
                for kc in range(8):
                    S.op("pe", lambda e, kc=kc, pb=pb, blk=blk: e.matmul(ps[pb][:, 0:256], lhsT=wmk[:, kc, blk * 128:(blk + 1) * 128],
                                                                       rhs=hTm[:, kc, :], start=(kc == 0), stop=(kc == 7)),
                         [("wmk", kc), ("hTm", kc)], ["ps%d" % pb])
                evac(mkT[:, blk, :], ps[pb][:, 0:256], ["ps%d" % pb], [("mkT", blk)])
            S.op("pool", lambda e: e.memset(memV, 1.0), [], ["memV"])
            for j in range(2):
                pb = 5 + j
                for kc in range(8):
                    S.op("pe", lambda e, kc=kc, pb=pb, j=j: e.matmul(ps[pb][:, 0:256], lhsT=hTm[:, kc, j * 128:(j + 1) * 128],
                                                                   rhs=wmk[:, kc, 256:512], start=(kc == 0), stop=(kc == 7)),
                         [("wmk", kc), ("hTm", kc)], ["ps%d" % pb])
                evac(memV[:, j, :, 0:64], ps[pb][:, 0:256].rearrange("p (h d) -> p h d", d=64), ["ps%d" % pb, "memV"],
                     [("memV", j)])
            S.barrier()
            A.top = mB

            cnts = {"ps": 0, "ips": 0, "p16": 0, "r16": 0, "acc": 0, "y": 0}
            deferred = []

            def flush_deferred(keep=0):
                while len(deferred) > keep:
                    for f in deferred.pop(0):
                        f()

            ALLBR = (("dsa", 3, 384, 384, 6, 6, 384, 0), ("moba", 3, 0, 0, 0, 0, 0, 1), ("mem", 2, 1280, 0, 0, 0, 768, 2))

            def load_q(mc, br):
                (mode, npair, qrow0, krow0, vidx0, tab0, ycol0, qi) = br
                qk = "qz%d" % qi
                qsrc = QT[qrow0:qrow0 + 128 * npair, 256 * mc:256 * mc + 256].rearrange("(b two p) t -> two p b t", two=2, p=64)
                S.dma("sp", qk + "a", lambda e: e.dma_start(out=qz[qi][0:64, 0:npair, 0:256], in_=qsrc[0]),
                      reads=["qzero%d" % qi], writes=[(qk, 0)])
                S.dma("sp", qk + "b", lambda e: e.dma_start(out=qz[qi][64:128, 0:npair, 256:512], in_=qsrc[1]),
                      reads=["qzero%d" % qi], writes=[(qk, 1)])

            def gate_part1(m):
                load_q(m, ALLBR[1])
                mqk = [("qz1", 0), ("qz1", 1)]
                for slot in range(2):
                    for hd in range(6):
                        p0 = (hd % 2) * 64
                        c0 = (hd % 2) * 256 + slot * 128
                        gb_ = 7 if hd % 2 == 0 else 4
                        gc = (slot * 3 + hd // 2) * 32
                        S.op("pe", lambda e, hd=hd, p0=p0, c0=c0, gb_=gb_, gc=gc: e.matmul(
                            ps[gb_][:, gc:gc + 32], lhsT=qz[1][p0:p0 + 64, hd // 2, c0:c0 + 128],
                            rhs=kmbf[p0:p0 + 64, hd // 2, :], start=True, stop=True), mqk + ["kmbf"], ["ps%d" % gb_])
                for par, gb_ in ((0, 7), (1, 4)):
                    S.op("act", lambda e, par=par, gb_=gb_: e.activation(out=gsb[:, par, :], in_=ps[gb_][:, 0:192], func=AF.Copy),
                         ["ps%d" % gb_], ["gp"])
                for slot in range(2):
                    for hd in range(6):
                        gc = (slot * 3 + hd // 2) * 32
                        S.op("dve", lambda e, hd=hd, gc=gc: e.tensor_tensor(
                            out=gm[:, hd, :], in0=gsb[:, hd % 2, gc:gc + 32], in1=pastneg_sb[:, m, :], op=ALU.add),
                             ["gp", "pastneg"], [("gm", hd)])
                        S.op("dve", lambda e, hd=hd: e.max(out=mx8[:, hd, :], in_=gm[:, hd, :]), [("gm", hd)], [("mx8", hd)])
                        S.op("dve", lambda e, hd=hd, slot=slot: e.scalar_tensor_tensor(
                            out=mbm[:, slot, hd, :], in0=gm[:, hd, :], scalar=mx8[:, hd, 2:3], in1=notown_sb[:, m, :],
                            op0=ALU.is_lt, op1=ALU.mult), [("gm", hd), ("mx8", hd), "notown"], [("mbm", slot, hd)])

            def gate_part2():
                if True:
                    for slot in range(2):
                        pTm = ps[7][:].bitcast(BF16)[0:32, 0:6 * 128].rearrange("p (h t) -> p h t", h=6)
                        for hd in range(6):
                            S.op("pe", lambda e, hd=hd, slot=slot, pTm=pTm: e.transpose(out=pTm[:, hd, :], in_=mbm[:, slot, hd, :],
                                                                                     identity=ident),
                                 [("mbm", slot, hd), "ident"], ["ps7"])
                        evac(mbmT[0:32, :, slot * 128:(slot + 1) * 128], pTm, ["ps7"], [("mbmT", slot)], eng="act")

            def gen_attention(m, modes, gate_for=None):
                L = 512 * (m + 1)
                nj = 4 * (m + 1)
                t0 = 256 * m
                pob = 5
                pok = "ps5"
                branches = [br for br in ALLBR if br[0] in modes]
                for br in branches:
                    if br[0] != "moba":
                        load_q(m, br)
                if gate_for is not None:
                    gate_part1(gate_for)
                first_pair = True
                for (mode, npair, qrow0, krow0, vidx0, tab0, ycol0, qi) in branches:
                    nh = 2 * npair
                    qkeys = [("qz%d" % qi, 0), ("qz%d" % qi, 1)]
                    yb_ = cnts["y"] % 2
                    cnts["y"] += 1
                    njj = 2 if mode == "mem" else nj
                    nseg = 1 if njj < 4 else 4
                    jps = njj // nseg
                    for b in range(npair):
                        if gate_for is not None and b == 1 and first_pair:
                            gate_part2()
                            first_pair = False
                        if mode != "mem":
                            flush_deferred()
                            for sg_ in range(nseg):
                                c0, c1 = sg_ * jps * 128, (sg_ + 1) * jps * 128
                                S.dma("sp", "khTs%d" % sg_, lambda e, b=b, c0=c0, c1=c1, krow0=krow0: e.dma_start(
                                    out=khT[:, c0:c1], in_=KT[krow0 + 128 * b:krow0 + 128 * b + 128, c0:c1]),
                                    writes=[("khT", j_) for j_ in range(sg_ * jps, (sg_ + 1) * jps)])
                                for wh in range(2):
                                    v0, v1 = sg_ * jps * 65, (sg_ + 1) * jps * 65
                                    S.dma("sp", "vhs%d_%d" % (wh, sg_), lambda e, b=b, wh=wh, v0=v0, v1=v1, vidx0=vidx0: e.dma_start(
                                        out=vh[wh][:, v0:v1], in_=VS[vidx0 + 2 * b + wh, :, v0:v1]),
                                        writes=[("vh", wh, j_) for j_ in range(sg_ * jps, (sg_ + 1) * jps)])
                        q_ap = qz[qi][:, b, :]
                        for j in range(njj):
                            sb_ = cnts["ips"] % 4
                            cnts["ips"] += 1
                            sk = "ps%d" % sb_
                            pS = ps[sb_][:, :]
                            has_tab = (mode != "mem") and (j >= nj - 5)
                            if mode == "mem":
                                S.op("pe", lambda e, b=b, j=j, pS=pS, q_ap=q_ap: e.matmul(
                                    pS, lhsT=mkT[:, b, j * 128:(j + 1) * 128], rhs=q_ap, start=True, stop=True),
                                     [("mkT", b)] + qkeys, [sk])
                            else:
                                S.op("pe", lambda e, j=j, pS=pS, q_ap=q_ap: e.matmul(
                                    pS, lhsT=khT[:, j * 128:(j + 1) * 128], rhs=q_ap, start=True, stop=False),
                                     [("khT", j)] + qkeys, [sk])
                                if mode == "dsa":
                                    S.op("pe", lambda e, j=j, pS=pS, ht=has_tab: e.matmul(
                                        pS, lhsT=ident, rhs=mbT[:, j:j + 1, :].broadcast_to([128, 2, 256]), start=False, stop=(not ht)),
                                         ["ident", ("mbT", j // 4, 0), ("mbT", j // 4, 1)], [sk])
                                else:
                                    S.op("pe", lambda e, b=b, j=j, pS=pS, ht=has_tab: e.matmul(
                                        pS, lhsT=Esb[:, j // 2, :], rhs=mbmT[:, 2 * b:2 * b + 2, :], start=False, stop=(not ht)),
                                         ["E", ("mbmT", 0), ("mbmT", 1)], [sk])
                                if has_tab:
                                    S.op("pe", lambda e, b=b, j=j, pS=pS, tab0=tab0: e.matmul(
                                        pS, lhsT=ident, rhs=tabs[:, tab0 + 2 * b:tab0 + 2 * b + 2, j - (nj - 5), :], start=False, stop=True),
                                         ["ident", ("tabs", tab0 + 2 * b), ("tabs", tab0 + 2 * b + 1)], [sk])
                            pe_ = cnts["p16"] % 3
                            cnts["p16"] += 1
                            pk_ = "p16_%d" % pe_
                            S.op("act", lambda e, sb_=sb_, pe_=pe_: e.activation(out=p16[pe_], in_=ps[sb_][:, :], func=AF.Exp), [sk], [pk_])
                            flush_deferred(keep=1)

                            def pv(j=j, pe_=pe_, pk_=pk_, b=b, njj=njj, mode=mode):
                                for wh in range(2):
                                    for slot in range(2):
                                        if mode == "mem":
                                            rhs = memV[:, j, 2 * b + wh, :]
                                            rk = [("memV", 0), ("memV", 1)]
                                        else:
                                            rhs = vh[wh][:, j * 65:(j + 1) * 65]
                                            rk = [("vh", wh, j)]
                                        q4 = wh * 2 + slot
                                        S.op("pe", lambda e, q4=q4, rhs=rhs: e.matmul(
                                            ps[pob][:, q4 * 65:(q4 + 1) * 65], lhsT=p16[pe_][:, q4 * 128:(q4 + 1) * 128],
                                            rhs=rhs, start=(j == 0 and q4 == 0), stop=(j == njj - 1), skip_group_check=True),
                                             [pk_] + rk, [pok])
                            entry = [pv]
                            if j == njj - 1:
                                def norm(b=b, yb_=yb_, ycol0=ycol0, nh=nh, npair=npair, mode=mode):
                                    prb = 0
                                    prk = "gp"
                                    S.op("act", lambda e: e.activation(out=praw[prb], in_=ps[pob][:, 0:260], func=AF.Copy), [pok], [prk])
                                    S.op("dve", lambda e: e.reciprocal(
                                        out=rden, in_=praw[prb].rearrange("p (q d) -> p q d", q=4)[:, :, 64]), [prk], ["rden"])
                                    for wh in range(2):
                                        for slot in range(2):
                                            q4 = wh * 2 + slot
                                            hd = 2 * b + wh
                                            S.op("dve", lambda e, slot=slot, q4=q4, hd=hd: e.tensor_scalar(
                                                out=ystg[yb_][:, slot, 64 * hd:64 * hd + 64], in0=praw[prb][:, q4 * 65:q4 * 65 + 64],
                                                scalar1=rden[:, q4:q4 + 1], scalar2=None, op0=ALU.mult),
                                                 [prk, "rden"], [("ystg", yb_, slot, hd)])
                                    if b == npair - 1:
                                        ykeys = [("ystg", yb_, s_, h_) for s_ in range(2) for h_ in range(nh)]
                                        S.dma("pool", "ystore%d" % yb_, lambda e: e.dma_start(
                                            out=YS[t0:t0 + 256, ycol0:ycol0 + 64 * nh].rearrange("(s p) d -> p s d", p=128),
                                            in_=ystg[yb_][:, :, 0:64 * nh]), reads=ykeys, writes=ykeys + [("YS", m, mode)])
                                entry.append(norm)
                            deferred.append(entry)
                            yield 0.65
                    if mode == "dsa":
                        yield ("dsa_done",)

            def gen_indexer(m):
                L = 512 * (m + 1)
                nsg = m + 1
                t0 = 256 * m
                S.dma("sp", "iq", lambda e: e.dma_start(
                    out=iq_sb[0:64], in_=QT[768:1280, t0:t0 + 256].rearrange("(h d) t -> d h t", d=64)), writes=["iq"])
                sckeys = [("sc", sg) for sg in range(nsg)]
                for slot in range(2):
                    sl = 2 * m + slot
                    for hh in range(8):
                        S.op("dve", lambda e, hh=hh, sl=sl: e.tensor_scalar(
                            out=dgw[:, hh, :], in0=ident, scalar1=iw_all[:, sl * 8 + hh:sl * 8 + hh + 1], scalar2=None,
                            op0=ALU.mult), ["ident", ("iw", sl)], [("dg", hh)])
                    for sg in range(nsg):
                        blk = sc[:, sg * 512:(sg + 1) * 512]
                        ab = (4, 6)[cnts["acc"] % 2]
                        cnts["acc"] += 1
                        ak = "ps%d" % ab
                        pq = []
                        for hh in range(8):
                            pb = cnts["ips"] % 4
                            cnts["ips"] += 1
                            rb = 3 + cnts["r16"] % 4
                            cnts["r16"] += 1
                            rk_ = "p16_%d" % rb
                            S.op("pe", lambda e, hh=hh, sg=sg, pb=pb, slot=slot: e.matmul(
                                ps[pb][:, :], lhsT=iq_sb[:, hh, slot * 128:(slot + 1) * 128], rhs=ikT[:, sg * 512:(sg + 1) * 512],
                                start=True, stop=True), ["iq", "ikT"], ["ps%d" % pb])
                            if hh % 2 == 0:
                                S.op("act", lambda e, pb=pb, rb=rb: e.activation(out=p16[rb], in_=ps[pb][:, :], func=AF.Relu),
                                     ["ps%d" % pb], [rk_])
                            else:
                                S.op("dve", lambda e, pb=pb, rb=rb: e.tensor_scalar(out=p16[rb], in0=ps[pb][:, :], scalar1=0.0,
                                                                                   scalar2=None, op0=ALU.max),
                                     ["ps%d" % pb], [rk_])

                            def dmm(hh=hh, rb=rb, rk_=rk_, ab=ab, ak=ak):
                                S.op("pe", lambda e: e.matmul(ps[ab][:, :], lhsT=dgw[:, hh, :], rhs=p16[rb], start=(hh == 0), stop=(hh == 7)),
                                     [("dg", hh), rk_], [ak])
                            pq.append(dmm)
                            while len(pq) > 3:
                                pq.pop(0)()
                            yield ("s", 0.45)
                        while pq:
                            pq.pop(0)()
                        S.op("act", lambda e, blk=blk, ab=ab: e.activation(out=blk, in_=ps[ab][:, :], func=AF.Copy), [ak], [("sc", sg)])
                        S.op("dve", lambda e, blk=blk, sg=sg: e.tensor_reduce(out=bmax[:, sg:sg + 1], in_=blk, axis=AX.X, op=ALU.max),
                             [("sc", sg)], [("bmax", sg)])
                        S.op("dve", lambda e, blk=blk, sg=sg: e.tensor_reduce(out=bmin[:, sg:sg + 1], in_=blk, axis=AX.X, op=ALU.min),
                             [("sc", sg)], [("bmin", sg)])
                    S.op("dve", lambda e: e.tensor_reduce(out=rmax, in_=bmax[:, 0:nsg], axis=AX.X, op=ALU.max),
                         [("bmax", g_) for g_ in range(nsg)], ["rmax"])
                    S.op("dve", lambda e: e.tensor_reduce(out=lo, in_=bmin[:, 0:nsg], axis=AX.X, op=ALU.min),
                         [("bmin", g_) for g_ in range(nsg)], ["lo"])
                    S.op("dve", lambda e, slot=slot: e.tensor_tensor(out=sc[:, L - 512:L], in0=sc[:, L - 512:L],
                                                                   in1=cmask_sb[:, slot, :], op=ALU.add),
                         ["cmask"], [("sc", nsg - 1)])
                    S.op("dve", lambda e: e.tensor_tensor(out=w0, in0=rmax, in1=lo, op=ALU.subtract), ["rmax", "lo"], ["w0"])
                    yield ("b", 1.0)
                    for k in range(NBIS):
                        c = 2.0 ** -(k + 1)
                        S.op("dve", lambda e, c=c: e.scalar_tensor_tensor(out=mid, in0=w0, scalar=c, in1=lo, op0=ALU.mult,
                                                                        op1=ALU.add), ["w0", "lo"], ["mid"])
                        S.op("dve", lambda e: e.tensor_scalar(out=mb[:, 0:L], in0=sc[:, 0:L], scalar1=mid, scalar2=0.0,
                                                             op0=ALU.is_ge, op1=ALU.add, accum_out=cnt),
                             sckeys + ["mid"], ["mb", "cnt"])
                        S.op("dve", lambda e, c=c: e.tensor_scalar(out=step, in0=cnt, scalar1=255.5, scalar2=c, op0=ALU.is_ge,
                                                                  op1=ALU.mult), ["cnt"], ["step"])
                        S.op("dve", lambda e: e.scalar_tensor_tensor(out=lo, in0=step, scalar=w0, in1=lo, op0=ALU.mult,
                                                                    op1=ALU.add), ["step", "w0", "lo"], ["lo"])
                        yield ("b", L * 1.1e-3 + 0.6)
                    S.op("dve", lambda e: e.tensor_scalar(out=mb[:, 0:L], in0=sc[:, 0:L], scalar1=lo, scalar2=NEG,
                                                         op0=ALU.is_lt, op1=ALU.mult), sckeys + ["lo"], ["mb"])
                    yield ("b", L * 0.6e-3)
                    yield ("maskT", slot)

            def gen_maskT(m, slot):
                nj = 4 * (m + 1)
                for jg in range(nj // 4):
                    tb_ = (7, 6)[jg % 2]
                    tk_ = "ps%d" % tb_
                    pT = ps[tb_][:].bitcast(BF16)[:, 0:512].rearrange("p (a t) -> p a t", a=4)
                    for jj in range(4):
                        j = 4 * jg + jj
                        S.op("pe", lambda e, jj=jj, j=j, pT=pT: e.transpose(out=pT[:, jj, :], in_=mb[:, j * 128:(j + 1) * 128],
                                                                          identity=ident), ["mb", "ident"], [tk_])
                    evac(mbT[:, 4 * jg:4 * jg + 4, slot * 128:(slot + 1) * 128], pT, [tk_], [("mbT", jg, slot)])
                    yield 0.7

            def emit_maskT(m, slot):
                for _ in gen_maskT(m, slot):
                    pass

            def drive(ga, gb, mb_chunk, a_total=0.0):
                Lb = 512 * (mb_chunk + 1)
                bis_total = 2 * (NBIS * (Lb * 1.1e-3 + 0.6) + 1.0 + Lb * 0.6e-3)
                score_total = 2 * (mb_chunk + 1) * 8 * 0.45
                ratio = max(0.0, (a_total - bis_total) / score_total) if gb is not None else 0.0
                ta = tb = 0.0
                a_done = ga is None
                dsa_done = ga is None
                b_done = gb is None
                b_wait = None
                while not (a_done and b_done):
                    if b_wait is not None and dsa_done:
                        if b_wait == 0 and not b_done:
                            for _ in gen_maskT(mb_chunk, 0):
                                for _k in range(2):
                                    try:
                                        c = next(gb)
                                        assert c[0] == "s", c
                                    except StopIteration:
                                        b_done = True
                                        break
                        else:
                            emit_maskT(mb_chunk, b_wait)
                        b_wait = None
                        continue
                    pick_a = (not a_done) and (b_done or b_wait is not None or ta < tb)
                    if pick_a:
                        try:
                            c = next(ga)
                            if isinstance(c, tuple):
                                dsa_done = True
                            else:
                                ta += c
                        except StopIteration:
                            a_done = True
                            dsa_done = True
                            flush_deferred()
                    else:
                        try:
                            c = next(gb)
                            if c[0] == "maskT":
                                b_wait = c[1]
                            elif c[0] == "s":
                                tb += c[1] * ratio
                            else:
                                tb += c[1]
                        except StopIteration:
                            b_done = True

            def chain(*gens):
                for g in gens:
                    if g is not None:
                        yield from g

            def gate_now(m):
                gate_part1(m)
                gate_part2()
                yield 5.0

            drive(chain(gate_now(0), gen_attention(0, ("moba", "mem"))), gen_indexer(0), 0, a_total=5.0 + 16 * 0.65)
            for m in range(nchunks):
                nxt = m + 1 if m + 1 < nchunks else None
                ga = chain(gen_attention(m, ("dsa",), gate_for=nxt),
                           gen_attention(nxt, ("moba", "mem")) if nxt is not None else None)
                a_total = 0.65 * (3 * 4 * (m + 1) + (3 * 4 * (m + 2) + 4 if nxt is not None else 0))
                drive(ga, gen_indexer(nxt) if nxt is not None else None, m + 1, a_total=a_total)
            S.barrier()
            A.top = m2

        if "p3" in phases:
            m3 = A.top
            wzg = A.bf16(8 * 4096).rearrange("p (k n) -> p k n", k=8)
            wbr = A.bf16(8 * D).rearrange("p (k n) -> p k n", k=8)
            wo = A.bf16(8 * D).rearrange("p (k n) -> p k n", k=8)
            fg = A.f32(D)
            mw = A.top
            wstg = [A.f32(4096) for _ in range(2)]
            load_weight(w_zg, 4096, wzg, gain_sb, "wzg", wstg)
            load_weight(w_br, D, wbr, None, "wbr", wstg)
            load_weight(w_o, D, wo, None, "wo", wstg)
            S.dma("sp", "s_fg", lambda e: e.dma_start(out=fg, in_=fgain), writes=["fg"])
            S.barrier()
            A.top = mw
            xt3 = [A.f32(D).rearrange("p (a d) -> p a d", a=1) for _ in range(2)]
            yt3 = [A.f32(D) for _ in range(2)]
            xs3 = A.bf16(D).rearrange("p (a d) -> p a d", a=1)
            junk3 = A.bf16(D)
            junk4 = A.bf16(D)
            hT3 = A.bf16(8 * 128).rearrange("p (k t) -> p k t", k=8)
            zg2 = [A.f32(4096) for _ in range(2)]
            u3 = A.bf16(D)
            uT = A.bf16(8 * 128).rearrange("p (k t) -> p k t", k=8)
            merged = A.f32(D)
            tmp3 = A.f32(512)
            mgb = A.bf16(D)
            mT = A.bf16(8 * 128).rearrange("p (k t) -> p k t", k=8)
            r3 = A.f32(D).rearrange("p (a d) -> p a d", a=1)
            ostg = [A.f32(D) for _ in range(2)]
            ss3 = A.f32(1)
            ms3 = A.f32(1)
            rs3 = A.f32(1)
            ss4 = A.f32(1)
            ms4 = A.f32(1)
            rs4 = A.f32(1)
            ntile3 = 2 * nchunks

            def transpose8(src, dst, skey, dkey, banks):
                for half in range(2):
                    pb = banks[half]
                    pT = ps[pb][:].bitcast(BF16)[:, 0:512].rearrange("p (a t) -> p a t", a=4)
                    for a in range(4):
                        kc = half * 4 + a
                        S.op("pe", lambda e, a=a, kc=kc, pT=pT: e.transpose(out=pT[:, a, :], in_=src[:, kc * 128:(kc + 1) * 128],
                                                                          identity=ident), skey + ["ident"], ["ps%d" % pb])
                    evac(dst[:, half * 4:half * 4 + 4, :], pT, ["ps%d" % pb], [(dkey, half)])

            def p3_front(i, part):
                b = i % 2
                xk = "x3_%d" % b
                yk = "y3_%d" % b
                zgb = zg2[b]
                if part == 0:
                    S.dma("sp", xk, lambda e: e.dma_start(out=xt3[b][:, 0, :], in_=xown[i * 128:(i + 1) * 128, :]), writes=[xk])
                    S.dma("sp", yk, lambda e: e.dma_start(out=yt3[b], in_=YS[i * 128:(i + 1) * 128, :]), writes=[yk])
                    rms_tile(xt3[b], 1, xs3, junk3, ss3, ms3, rs3, [xk], "p3")
                    transpose8(xs3[:, 0, :], hT3, [("p3xs", 0)], "hT3", (6, 7))
                hkeys = [("hT3", 0), ("hT3", 1)]
                for grp in range(4 * part, 4 * part + 4):
                    pb = grp % 3
                    for kc in range(8):
                        S.op("pe", lambda e, kc=kc, pb=pb, grp=grp: e.matmul(
                            ps[pb][:, :], lhsT=hT3[:, kc, :], rhs=wzg[:, kc, grp * 512:(grp + 1) * 512],
                            start=(kc == 0), stop=(kc == 7)), [("wzg", kc)] + hkeys, ["ps%d" % pb])
                    fn = AF.Silu if grp < 2 else AF.Sigmoid
                    S.op("act", lambda e, pb=pb, grp=grp, fn=fn: e.activation(out=zgb[:, grp * 512:(grp + 1) * 512], in_=ps[pb][:, :],
                                                                           func=fn), ["ps%d" % pb], [("zg", b, grp)])

            def p3_back(i, part):
                b = i % 2
                xk = "x3_%d" % b
                yk = "y3_%d" % b
                zgb = zg2[b]
                if part == 0:
                    p3_back_a(i, b, xk, yk, zgb)
                else:
                    p3_back_b(i, b, xk, yk, zgb)

            def p3_back_a(i, b, xk, yk, zgb):
                for half in range(2):
                    S.op("dve", lambda e, half=half: e.tensor_tensor(
                        out=u3[:, half * 512:(half + 1) * 512], in0=yt3[b][:, half * 512:(half + 1) * 512],
                        in1=zgb[:, half * 512:(half + 1) * 512], op=ALU.mult), [yk, ("zg", b, half)], [("u3", half)])
                transpose8(u3, uT, [("u3", 0), ("u3", 1)], "uT", (5, 4))
                ukeys = [("uT", 0), ("uT", 1)]
                blks = [(0, 3), (3, 6), (6, 8)]
                for dgi in range(2):
                    for bi, (b0, b1) in enumerate(blks):
                        pb = 3 + (dgi * 3 + bi) % 3
                        for kb in range(b0, b1):
                            S.op("pe", lambda e, kb=kb, pb=pb, dgi=dgi, b0=b0, b1=b1: e.matmul(
                                ps[pb][:, :], lhsT=uT[:, kb, :], rhs=wbr[:, kb, dgi * 512:(dgi + 1) * 512],
                                start=(kb == b0), stop=(kb == b1 - 1)), [("wbr", kb)] + ukeys, ["ps%d" % pb])
                        sg_ap = zgb[:, 1024 + bi * 1024 + dgi * 512:1024 + bi * 1024 + (dgi + 1) * 512]
                        sgk = ("zg", b, 2 + bi * 2 + dgi)
                        mg_ap = merged[:, dgi * 512:(dgi + 1) * 512]
                        if bi == 0:
                            S.op("dve", lambda e, pb=pb, sg_ap=sg_ap, mg_ap=mg_ap: e.tensor_tensor(
                                out=mg_ap, in0=ps[pb][:, :], in1=sg_ap, op=ALU.mult), ["ps%d" % pb, sgk], [("mg", dgi)])
                        else:
                            S.op("dve", lambda e, pb=pb, sg_ap=sg_ap: e.tensor_tensor(
                                out=tmp3, in0=ps[pb][:, :], in1=sg_ap, op=ALU.mult), ["ps%d" % pb, sgk], ["tmp3"])
                            o_ap = mg_ap if bi == 1 else mgb[:, dgi * 512:(dgi + 1) * 512]
                            okey = ("mg", dgi) if bi == 1 else ("mgb", dgi)
                            S.op("dve", lambda e, mg_ap=mg_ap, o_ap=o_ap: e.tensor_tensor(
                                out=o_ap, in0=mg_ap, in1=tmp3, op=ALU.add), ["tmp3", ("mg", dgi)], [okey])

            def p3_back_b(i, b, xk, yk, zgb):
                transpose8(mgb, mT, [("mgb", 0), ("mgb", 1)], "mT", (5, 4))
                mkeys = [("mT", 0), ("mT", 1)]
                for eg in range(2):
                    pb = 3 + eg
                    for kc in range(8):
                        S.op("pe", lambda e, kc=kc, pb=pb, eg=eg: e.matmul(
                            ps[pb][:, :], lhsT=mT[:, kc, :], rhs=wo[:, kc, eg * 512:(eg + 1) * 512],
                            start=(kc == 0), stop=(kc == 7)), [("wo", kc)] + mkeys, ["ps%d" % pb])
                    S.op("dve", lambda e, pb=pb, eg=eg: e.tensor_tensor(
                        out=r3[:, 0, eg * 512:(eg + 1) * 512], in0=ps[pb][:, :], in1=xt3[b][:, 0, eg * 512:(eg + 1) * 512],
                        op=ALU.add), ["ps%d" % pb, xk], [("r3", eg)])
                rms_tile(r3, 1, None, junk4, ss4, ms4, rs4, [("r3", 0), ("r3", 1)], "p3f")
                ok = "ostg%d" % b
                S.op("dve", lambda e: e.scalar_tensor_tensor(out=ostg[b], in0=r3[:, 0, :], scalar=rs4[:, 0:1], in1=fg,
                                                            op0=ALU.mult, op1=ALU.mult),
                     [("r3", 0), ("r3", 1), "p3frstd", "fg"], [ok])
                S.dma("pool", ok, lambda e: e.dma_start(out=out[i * 128:(i + 1) * 128, :], in_=ostg[b]),
                      reads=[ok], writes=[("out", i)])

            p3_front(0, 0)
            p3_front(0, 1)
            for i in range(ntile3):
                if i + 1 < ntile3:
                    p3_front(i + 1, 0)
                p3_back(i, 0)
                if i + 1 < ntile3:
                    p3_front(i + 1, 1)
                p3_back(i, 1)
            A.top = m3
        S.barrier()
        with nc.Block() as block:
            S.emit(block)
    return nc


def np_bucket(dist):
    n = np.maximum(dist, 0)
    nf = np.maximum(n, 1).astype(np.float32)
    large = 16 + (np.log(nf / np.float32(16)) / np.float32(math.log(128 / 16)) * np.float32(16)).astype(np.int32)
    return np.where(n < 16, n, np.minimum(large, 31))


def own_rows(h):
    return np.concatenate([np.arange(256 * (2 * m + h), 256 * (2 * m + h) + 256) for m in range(NCH)])


def make_in_maps(x, mem, norm_gain, w_in, rel_bias, mem_norm_gain, w_mem_kv, w_branch, w_out, final_norm_gain):
    f = np.float32
    w = np.asarray(w_in[0], f)

    def cols(*names):
        return np.ascontiguousarray(np.concatenate([w[:, OFF[n][0]:OFF[n][1]] for n in names], axis=1))

    shared = {
        "gain": np.ascontiguousarray(np.asarray(norm_gain[0], f).reshape(8, 128).T),
        "mgain": np.ascontiguousarray(np.asarray(mem_norm_gain[0], f).reshape(8, 128).T),
        "fgain": np.ascontiguousarray(np.broadcast_to(np.asarray(final_norm_gain, f)[None, :], (128, D))),
        "w_kvT": cols("ka", "kb", "ik"),
        "w_v": cols("va", "vb"),
        "w_qT": cols("qa", "qb", "iq", "qm"),
        "w_iw": cols("iw"),
        "w_zg": cols("za", "zb", "zm", "ga", "gb", "gm"),
        "w_mkv": np.ascontiguousarray(np.asarray(w_mem_kv[0], f)),
        "w_br": np.ascontiguousarray(np.asarray(w_branch[0], f)),
        "w_o": np.ascontiguousarray(np.asarray(w_out[0], f)),
        "b31": np.ascontiguousarray(np.broadcast_to(np.asarray(rel_bias, f)[31][None, :], (128, 12))),
        "ident": np.eye(128, dtype=f).astype(ml_dtypes.bfloat16),
    }
    E = np.zeros((32, 32, 128), f)
    for n in range(32):
        E[n, n, :] = 1.0
    shared["E"] = E.reshape(32, 32 * 128).astype(ml_dtypes.bfloat16)
    rb = np.asarray(rel_bias, f)
    sl = np.arange(128)[:, None]
    tl = np.arange(256)[None, :]
    maps = []
    for c in range(8):
        b, h = c // 2, c % 2
        rows = own_rows(h)
        m = dict(shared)
        m["xall"] = np.ascontiguousarray(np.asarray(x[b], f))
        m["xown"] = np.ascontiguousarray(np.asarray(x[b], f)[rows])
        m["memx"] = np.ascontiguousarray(np.asarray(mem[b], f))
        tabraw = np.zeros((12, 128, 5, 256), f)
        cmt = np.zeros((128, 5, 256), f)
        for ii, i in enumerate(range(1, 6)):
            Di = 256 * h + 256 - 128 * i
            d = Di + tl - sl
            bk = np_bucket(d)
            tabraw[:, :, ii, :] = rb[bk].transpose(2, 0, 1)
            cmt[:, ii, :] = np.where(d < 0, NEG, 0.0)
        m["tabraw"] = tabraw
        m["cmt"] = cmt
        cm = np.zeros((128, 2, 512), f)
        k = np.arange(512)[None, :]
        for slot in range(2):
            lim = 256 * h + 128 * slot + np.arange(128)[:, None]
            cm[:, slot, :] = np.where(k <= lim, 0.0, NEG)
        m["cmask"] = cm.astype(ml_dtypes.bfloat16)
        pn = np.zeros((128, 16, 32), f)
        no = np.full((128, 16, 32), NEG, f)
        for mm in range(16):
            own = 2 * mm + h
            pn[:, mm, own:] = NEG
            no[:, mm, own] = 0.0
        m["pastneg"] = pn.astype(ml_dtypes.bfloat16)
        m["notown"] = no.astype(ml_dtypes.bfloat16)
        maps.append(m)
    return maps


_NC_CACHE = {}


def kernel(x, mem, norm_gain, w_in, rel_bias, mem_norm_gain, w_mem_kv, w_branch, w_out, final_norm_gain):
    maps = make_in_maps(x, mem, norm_gain, w_in, rel_bias, mem_norm_gain, w_mem_kv, w_branch, w_out, final_norm_gain)
    if "nc" not in _NC_CACHE:
        _NC_CACHE["nc"] = build_program()
    res = run_bass_kernel_spmd(_NC_CACHE["nc"], maps, core_ids=list(range(8)))
    outf = np.zeros((4, T, D), np.float32)
    for c in range(8):
        b, h = c // 2, c % 2
        outf[b, own_rows(h)] = res.results[c]["out"]
    return outf
```
